# Optimizing a Trainium2 kernel written in Bass

```python
import math
import jax, jax.numpy as jnp
from jax import lax
import numpy as np

D_MODEL = 1024
BATCH = 8
SEQ = 8192
DEPTH = 4

GRID_W = 64
CTX_LEN = 256
N_MIXERS = 2
N_A = (DEPTH + 1) // 2
N_B = DEPTH // 2
D_MIX = D_MODEL
CHUNK = 128
CM_GROUPS = 8
CM_GROUP_DIM = D_MIX // CM_GROUPS
SSM_GROUP = 16
SSM_GROUPS = D_MIX // SSM_GROUP
SSM_STATE = 64
PEER_HEADS = 8
PEER_KEYS = 128
PEER_EXPERTS = PEER_KEYS * PEER_KEYS
PEER_TOPK = 16
PEER_DKEY = 512
PEER_DHALF = PEER_DKEY // 2
PEER_BLOCK = 128
N_MOD = 6
EPS = 1e-6

kernel_name = "hybrid_gmlp_s5_peer_dit"


def _rmsnorm(x, g):
    x32 = x.astype(jnp.float32)
    y = x32 * lax.rsqrt(jnp.mean(x32 * x32, axis=-1, keepdims=True) + EPS)
    return (y * g.astype(jnp.float32)).astype(x.dtype)


def _layernorm(x, g, b):
    x32 = x.astype(jnp.float32)
    mu = jnp.mean(x32, axis=-1, keepdims=True)
    xc = x32 - mu
    y = xc * lax.rsqrt(jnp.mean(xc * xc, axis=-1, keepdims=True) + EPS)
    return (y * g.astype(jnp.float32) + b.astype(jnp.float32)).astype(x.dtype)


def _modulate(h, shift, scale):
    return h * (1.0 + scale) + shift


def _sincos_1d(pos, dim):
    omega = 1.0 / (10000.0 ** (jnp.arange(dim // 2, dtype=jnp.float32) / (dim // 2)))
    ang = pos[:, None] * omega[None, :]
    return jnp.concatenate([jnp.sin(ang), jnp.cos(ang)], axis=-1)


def _pos_embed_2d(rows, dim):
    r = jnp.repeat(jnp.arange(rows, dtype=jnp.float32), GRID_W)
    col = jnp.tile(jnp.arange(GRID_W, dtype=jnp.float32), rows)
    return jnp.concatenate([_sincos_1d(r, dim // 2), _sincos_1d(col, dim // 2)], axis=-1)


def _chunk_mixer(h, w_in, ln_g, ln_b, w_s, b_s, w_out):
    bsz, length, _ = h.shape
    z = jax.nn.gelu(h @ w_in)
    u, v = jnp.split(z, 2, axis=-1)
    v = _layernorm(v, ln_g, ln_b)
    v = v.reshape(bsz, length // CHUNK, CHUNK, CM_GROUPS, CM_GROUP_DIM)
    v = jnp.einsum("hpq,bnqhd->bnphd", w_s, v) + b_s.T[:, :, None]
    return (u * v.reshape(bsz, length, D_MIX)) @ w_out


def _cmul(ar, ai, br, bi):
    return ar * br - ai * bi, ar * bi + ai * br


def _scan_combine(e1, e2):
    a1r, a1i, b1r, b1i = e1
    a2r, a2i, b2r, b2i = e2
    ar, ai = _cmul(a2r, a2i, a1r, a1i)
    br, bi = _cmul(a2r, a2i, b1r, b1i)
    return ar, ai, br + b2r, bi + b2i


def _diag_scan(a_r, a_i, bu_r, bu_i, h0=None):
    if h0 is not None:
        inj_r, inj_i = _cmul(a_r, a_i, h0[0], h0[1])
        bu_r = bu_r.at[0].add(inj_r)
        bu_i = bu_i.at[0].add(inj_i)
    a_full_r = jnp.broadcast_to(a_r, bu_r.shape)
    a_full_i = jnp.broadcast_to(a_i, bu_i.shape)
    _, _, h_r, h_i = lax.associative_scan(_scan_combine, (a_full_r, a_full_i, bu_r, bu_i), axis=0)
    return h_r, h_i


def _ssm_direction(u_lat, u_ctx, a_r, a_i, bb_r, bb_i, c_r, c_i, need_ctx_out):
    def inject(u):
        return (jnp.einsum("lgc,gpc->lgp", u, bb_r), jnp.einsum("lgc,gpc->lgp", u, bb_i))

    def readout(h_r, h_i):
        return jnp.einsum("lgp,gcp->lgc", h_r, c_r) - jnp.einsum("lgp,gcp->lgc", h_i, c_i)

    hc_r, hc_i = _diag_scan(a_r, a_i, *inject(u_ctx))
    hx_r, hx_i = _diag_scan(a_r, a_i, *inject(u_lat), h0=(hc_r[-1], hc_i[-1]))
    if need_ctx_out:
        return readout(hx_r, hx_i), readout(hc_r, hc_i)
    return readout(hx_r, hx_i), None


def _ssm_mixer(xn, cn, w_in, lam_re, lam_im, log_step, b_re, b_im, c_re, c_im, d_skip, w_out,
               need_ctx_out):
    u_x = xn @ w_in
    u_c = cn @ w_in
    lr = lam_re.astype(jnp.float32)
    li = lam_im.astype(jnp.float32)
    step = jnp.exp(log_step.astype(jnp.float32))[..., None]
    mag = jnp.exp(lr * step)
    a_r = mag * jnp.cos(li * step)
    a_i = mag * jnp.sin(li * step)
    den = lr * lr + li * li
    coef_r = ((a_r - 1.0) * lr + a_i * li) / den
    coef_i = (a_i * lr - (a_r - 1.0) * li) / den
    bb_r, bb_i = _cmul(coef_r[..., None], coef_i[..., None],
                       b_re.astype(jnp.float32), b_im.astype(jnp.float32))
    c_r = c_re.astype(jnp.float32)
    c_i = c_im.astype(jnp.float32)

    def per_sample(args):
        ux, uc = args
        ux = ux.astype(jnp.float32).reshape(-1, SSM_GROUPS, SSM_GROUP)
        uc = uc.astype(jnp.float32).reshape(-1, SSM_GROUPS, SSM_GROUP)
        yf_x, yf_c = _ssm_direction(ux, uc, a_r[0], a_i[0], bb_r[0], bb_i[0], c_r[0], c_i[0],
                                    need_ctx_out)
        yb_x, yb_c = _ssm_direction(ux[::-1], uc[::-1], a_r[1], a_i[1], bb_r[1], bb_i[1],
                                    c_r[1], c_i[1], need_ctx_out)
        y_lat = yf_x + yb_x[::-1]
        if need_ctx_out:
            return y_lat, yf_c + yb_c[::-1]
        return y_lat

    out = lax.map(per_sample, (u_x, u_c))

    def finish(y, u):
        y = y.reshape(u.shape).astype(u.dtype) + d_skip * u
        val, gate = jnp.split(jax.nn.gelu(y) @ w_out, 2, axis=-1)
        return val * jax.nn.sigmoid(gate)

    if need_ctx_out:
        return finish(out[0], u_x), finish(out[1], u_c)
    return finish(out, u_x), None


def _peer(h, w_q, sub_keys, u_tab, v_tab):
    shape = h.shape
    blocks = h.reshape(-1, PEER_BLOCK, D_MODEL)

    def block(hb):
        q = (hb @ w_q).reshape(PEER_BLOCK, PEER_HEADS, 2, PEER_DHALF)
        s = jnp.einsum("nhkd,hkmd->nhkm", q.astype(jnp.float32), sub_keys.astype(jnp.float32))
        s_half, i_half = lax.top_k(s, PEER_TOPK)
        cand = s_half[:, :, 0, :, None] + s_half[:, :, 1, None, :]
        top_s, top_c = lax.top_k(cand.reshape(PEER_BLOCK, PEER_HEADS, PEER_TOPK * PEER_TOPK),
                                 PEER_TOPK)
        k1 = jnp.take_along_axis(i_half[:, :, 0], top_c // PEER_TOPK, axis=-1)
        k2 = jnp.take_along_axis(i_half[:, :, 1], top_c % PEER_TOPK, axis=-1)
        ids = k1 * PEER_KEYS + k2
        g = jax.nn.softmax(top_s, axis=-1)
        ue = jnp.take(u_tab, ids, axis=0)
        ve = jnp.take(v_tab, ids, axis=0)
        act = jax.nn.gelu(jnp.einsum("nd,nhkd->nhk", hb, ue).astype(jnp.float32))
        return jnp.einsum("nhk,nhkd->nd", (g * act).astype(hb.dtype), ve)

    return lax.map(block, blocks).reshape(shape)


def setup_inputs(seed: int = 0) -> dict:
    key = jax.random.key(seed)
    ks = jax.random.split(key, 32)
    f32 = jnp.float32
    nrm = lambda k, s, std: jax.random.normal(k, s, f32) * std
    G, P = SSM_GROUPS, SSM_STATE
    lam_im0 = jnp.broadcast_to(jnp.pi * jnp.arange(P, dtype=f32), (N_B, 2, G, P))
    return {
        "x": nrm(ks[0], (BATCH, SEQ, D_MODEL), 1.0),
        "c": nrm(ks[1], (BATCH, D_MODEL), 1.0),
        "ctx": nrm(ks[2], (BATCH, CTX_LEN, D_MODEL), 1.0),
        "c_ctx": nrm(ks[3], (D_MODEL,), 1.0),
        "ada_w": nrm(ks[4], (DEPTH, D_MODEL, N_MOD * D_MODEL), 0.5 * D_MODEL ** -0.5),
        "ada_b": nrm(ks[5], (DEPTH, N_MOD * D_MODEL), 0.02),
        "norm1_g": 1.0 + nrm(ks[6], (DEPTH, D_MODEL), 0.02),
        "norm2_g": 1.0 + nrm(ks[7], (DEPTH, D_MODEL), 0.02),
        "final_g": 1.0 + nrm(ks[8], (D_MODEL,), 0.02),
        "cm_w_in": nrm(ks[9], (N_A, D_MODEL, 2 * D_MIX), D_MODEL ** -0.5),
        "cm_ln_g": 1.0 + nrm(ks[10], (N_A, D_MIX), 0.02),
        "cm_ln_b": nrm(ks[11], (N_A, D_MIX), 0.02),
        "cm_w_s": nrm(ks[12], (N_A, CM_GROUPS, CHUNK, CHUNK), CHUNK ** -0.5),
        "cm_b_s": 1.0 + nrm(ks[13], (N_A, CM_GROUPS, CHUNK), 0.02),
        "cm_w_out": nrm(ks[14], (N_A, D_MIX, D_MODEL), D_MIX ** -0.5),
        "ssm_w_in": nrm(ks[15], (N_B, D_MODEL, D_MIX), D_MODEL ** -0.5),
        "ssm_lam_re": -0.5 + nrm(ks[16], (N_B, 2, G, P), 0.01),
        "ssm_lam_im": lam_im0 + nrm(ks[17], (N_B, 2, G, P), 0.01),
        "ssm_log_step": jax.random.uniform(ks[18], (N_B, 2, G), f32,
                                           minval=math.log(1e-3), maxval=math.log(1e-1)),
        "ssm_b_re": nrm(ks[19], (N_B, 2, G, P, SSM_GROUP), (2 * SSM_GROUP) ** -0.5),
        "ssm_b_im": nrm(ks[20], (N_B, 2, G, P, SSM_GROUP), (2 * SSM_GROUP) ** -0.5),
        "ssm_c_re": nrm(ks[21], (N_B, 2, G, SSM_GROUP, P), P ** -0.5),
        "ssm_c_im": nrm(ks[22], (N_B, 2, G, SSM_GROUP, P), P ** -0.5),
        "ssm_d": nrm(ks[23], (N_B, D_MIX), 1.0),
        "ssm_w_out": nrm(ks[24], (N_B, D_MIX, 2 * D_MODEL), D_MIX ** -0.5),
        "peer_w_q": nrm(ks[25], (DEPTH, D_MODEL, PEER_HEADS * PEER_DKEY), D_MODEL ** -0.5),
        "peer_keys": nrm(ks[26], (DEPTH, PEER_HEADS, 2, PEER_KEYS, PEER_DHALF), PEER_DHALF ** -0.5),
        "peer_u": nrm(ks[27], (DEPTH, PEER_EXPERTS, D_MODEL), D_MODEL ** -0.5),
        "peer_v": nrm(ks[28], (DEPTH, PEER_EXPERTS, D_MODEL), (PEER_HEADS * PEER_TOPK) ** -0.5),
    }


def reference(x, c, ctx, c_ctx, ada_w, ada_b, norm1_g, norm2_g, final_g,
              cm_w_in, cm_ln_g, cm_ln_b, cm_w_s, cm_b_s, cm_w_out,
              ssm_w_in, ssm_lam_re, ssm_lam_im, ssm_log_step, ssm_b_re, ssm_b_im,
              ssm_c_re, ssm_c_im, ssm_d, ssm_w_out,
              peer_w_q, peer_keys, peer_u, peer_v):
    rows = x.shape[1] // GRID_W
    x = x + _pos_embed_2d(rows, D_MODEL).astype(x.dtype)[None]
    h_ctx = ctx
    silu_c = jax.nn.silu(c)
    silu_cc = jax.nn.silu(c_ctx)
    for i in range(DEPTH):
        last = i == DEPTH - 1
        k = i // N_MIXERS
        use_chunk = (i % N_MIXERS) == 0
        need_ctx_mix = (not last) or (not use_chunk)
        mod_x = (silu_c @ ada_w[i] + ada_b[i])[:, None, :]
        mod_c = (silu_cc @ ada_w[i] + ada_b[i])[None, None, :]
        sh1, sc1, g1, sh2, sc2, g2 = jnp.split(mod_x, N_MOD, axis=-1)
        csh1, csc1, cg1, csh2, csc2, cg2 = jnp.split(mod_c, N_MOD, axis=-1)

        xn = _modulate(_rmsnorm(x, norm1_g[i]), sh1, sc1)
        cn = _modulate(_rmsnorm(h_ctx, norm1_g[i]), csh1, csc1) if need_ctx_mix else None
        if use_chunk:
            y_x = _chunk_mixer(xn, cm_w_in[k], cm_ln_g[k], cm_ln_b[k], cm_w_s[k], cm_b_s[k],
                               cm_w_out[k])
            y_c = None if last else _chunk_mixer(cn, cm_w_in[k], cm_ln_g[k], cm_ln_b[k],
                                                 cm_w_s[k], cm_b_s[k], cm_w_out[k])
        else:
            y_x, y_c = _ssm_mixer(xn, cn, ssm_w_in[k], ssm_lam_re[k], ssm_lam_im[k],
                                  ssm_log_step[k], ssm_b_re[k], ssm_b_im[k], ssm_c_re[k],
                                  ssm_c_im[k], ssm_d[k], ssm_w_out[k], not last)
        x = x + g1 * y_x
        x = x + g2 * _peer(_modulate(_rmsnorm(x, norm2_g[i]), sh2, sc2),
                           peer_w_q[i], peer_keys[i], peer_u[i], peer_v[i])
        if not last:
            h_ctx = h_ctx + cg1 * y_c
            h_ctx = h_ctx + cg2 * _peer(_modulate(_rmsnorm(h_ctx, norm2_g[i]), csh2, csc2),
                                        peer_w_q[i], peer_keys[i], peer_u[i], peer_v[i])
    return _rmsnorm(x, final_g)
```

```python
import numpy as np
from contextlib import ExitStack
from types import SimpleNamespace
import concourse.bass as bass
import concourse.mybir as mybir
from concourse.bass_utils import run_bass_kernel_spmd

F32 = mybir.dt.float32
BF16 = mybir.dt.bfloat16
AF = mybir.ActivationFunctionType
ALU = mybir.AluOpType
AX = mybir.AxisListType
D = 1024
NEG = -1.0e30
EPS = 1e-6
PI = float(np.pi)


def _rows(a):
    a = np.ascontiguousarray(a, dtype=np.float32).reshape(-1)
    pad = (-a.size) % D
    if pad:
        a = np.concatenate([a, np.zeros(pad, np.float32)])
    return a.reshape(-1, D)


def _pos_embed(rows):
    def sincos(pos, dim):
        omega = (1.0 / (np.float32(10000.0) ** (np.arange(dim // 2, dtype=np.float32) / np.float32(dim // 2)))).astype(np.float32)
        ang = (pos[:, None] * omega[None, :]).astype(np.float32)
        return np.concatenate([np.sin(ang), np.cos(ang)], axis=-1).astype(np.float32)
    r = np.repeat(np.arange(rows, dtype=np.float32), 64)
    col = np.tile(np.arange(64, dtype=np.float32), rows)
    return np.concatenate([sincos(r, D // 2), sincos(col, D // 2)], axis=-1).astype(np.float32)


def pack_weights(inp, layers, n_lat):
    parts, off = [], {}
    cur = 0

    def add(name, arr):
        nonlocal cur
        r = _rows(arr)
        off[name] = cur
        parts.append(r)
        cur += r.shape[0]

    add("pos", _pos_embed(n_lat // 64))
    add("final_g", inp["final_g"])
    for i in layers:
        k = i // 2
        add(f"ada_w{i}", inp["ada_w"][i].reshape(D, 6, D).transpose(1, 0, 2))
        add(f"ada_b{i}", inp["ada_b"][i])
        add(f"n1g{i}", inp["norm1_g"][i])
        add(f"n2g{i}", inp["norm2_g"][i])
        add(f"wqT{i}", inp["peer_w_q"][i].T)
        add(f"keysT{i}", inp["peer_keys"][i].transpose(0, 1, 3, 2))
        add(f"uT{i}", inp["peer_u"][i].T.reshape(D, 16, D).transpose(1, 0, 2))
        add(f"v{i}", inp["peer_v"][i])
        if i % 2 == 0:
            add(f"cm_w_in{i}", inp["cm_w_in"][k].reshape(D, 2, D).transpose(1, 0, 2))
            add(f"cm_w_out{i}", inp["cm_w_out"][k])
            add(f"cm_ln_g{i}", inp["cm_ln_g"][k])
            add(f"cm_ln_b{i}", inp["cm_ln_b"][k])
            add(f"cm_wsT{i}", inp["cm_w_s"][k].transpose(0, 2, 1))
            add(f"cm_bs{i}", inp["cm_b_s"][k])
        else:
            add(f"ssm_w_in{i}", inp["ssm_w_in"][k])
            add(f"ssm_w_out{i}", inp["ssm_w_out"][k].reshape(D, 2, D).transpose(1, 0, 2))
            def sp(a):
                return a.reshape(2, 32, 128).transpose(0, 2, 1)
            add(f"lamre{i}", sp(inp["ssm_lam_re"][k]))
            add(f"lamim{i}", sp(inp["ssm_lam_im"][k]))
            add(f"lstep{i}", sp(np.broadcast_to(inp["ssm_log_step"][k][:, :, None], (2, 64, 64))))
            def bT(b):
                o = np.zeros((2, 128, 8, 4, 128), np.float32)
                for g in range(64):
                    ch0 = g * 16
                    chunk, cl = ch0 // 128, ch0 % 128
                    o[:, cl:cl + 16, chunk, (g // 2) % 4, (g % 2) * 64:(g % 2) * 64 + 64] = b[:, g].transpose(0, 2, 1)
                return o
            add(f"bT_r{i}", bT(inp["ssm_b_re"][k]))
            add(f"bT_i{i}", bT(inp["ssm_b_im"][k]))
            def cT(c):
                o = np.zeros((2, 128, 32, 32), np.float32)
                for g in range(64):
                    o[:, (g % 2) * 64:(g % 2) * 64 + 64, g // 2, (g % 2) * 16:(g % 2) * 16 + 16] = c[:, g].transpose(0, 2, 1)
                return o
            add(f"cT_r{i}", cT(inp["ssm_c_re"][k]))
            add(f"cT_i{i}", cT(inp["ssm_c_im"][k]))
            add(f"ssm_d{i}", inp["ssm_d"][k])
    pad = (-cur) % 8
    if pad:
        parts.append(np.zeros((pad, D), np.float32))
        cur += pad
    return np.concatenate(parts, axis=0), off, cur


def _is_ap(x):
    return hasattr(x, "ap") and hasattr(x, "offset") and hasattr(x, "space") and hasattr(x, "tensor")


def _region(a):
    steps = list(a.ap)
    sp = str(a.space)
    if sp in ("SB", "PSUM"):
        shape = list(a.tensor.shape)
        row = 1
        for d in shape[1:]:
            row *= d
        if steps and (steps[0][0] == row or steps[0][1] == 1):
            base = a.offset % row
            dims = steps[1:]
        else:
            return (a.name, 0, row - 1)
    else:
        base = a.offset
        dims = steps
    lo = base + sum(min(0, st * (c - 1)) for st, c in dims)
    hi = base + sum(max(0, st * (c - 1)) for st, c in dims)
    if sp == "PSUM":
        lo = (lo // 512) * 512
        hi = (hi // 512) * 512 + 511
    return (a.name, lo, hi)


class _Proxy:
    WRITE_KEYS = ("out", "accum_out", "out_max", "out_indices")

    def __init__(self, prog, eng):
        self.prog, self.eng = prog, eng

    def __getattr__(self, name):
        prog, eng = self.prog, self.eng
        real = getattr(getattr(prog.nc, eng), name)

        def call(*args, **kw):
            reads, writes = [], []
            for j, x in enumerate(args):
                if _is_ap(x):
                    (writes if j == 0 else reads).append(x)
            for k, x in kw.items():
                if _is_ap(x):
                    (writes if k in self.WRITE_KEYS else reads).append(x)
            prog._sync(eng, reads, writes)
            return real(*args, **kw)
        return call


class Prog:
    ENG = ("sync", "scalar", "vector", "gpsimd", "tensor")
    NDMA = 8

    def __init__(self, nc, stack, readonly=()):
        self.nc, self.stack = nc, stack
        self.sem, self.cnt, self.nsem = {}, {}, 0
        self.waited = {e: {} for e in self.ENG}
        self.Wr, self.Rd = {}, {}
        self.readonly = set(readonly)
        self.dpool, self.dcnt, self.dk = {}, {}, {}
        self.pending = None
        self.last = None

    def _newsem(self):
        s = self.stack.enter_context(self.nc.semaphore(f"s{self.nsem}"))
        self.nsem += 1
        return (self.nsem - 1, s)

    def _need(self, eng, ev, need):
        if ev is None:
            return
        (sid, sem), val, peng = ev
        if peng == "tensor" and eng == "tensor":
            return
        if need.get(sid, (None, 0))[1] < val:
            need[sid] = (sem, val)

    def _sync(self, eng, reads, writes):
        need = {}
        rr = [_region(a) for a in reads if str(a.space) != "PSUM"]
        ww = [_region(a) for a in writes] + [_region(a) for a in reads if str(a.space) == "PSUM"]
        for (nm, lo, hi) in rr:
            for (l2, h2, ev) in self.Wr.get(nm, ()):
                if l2 <= hi and lo <= h2:
                    self._need(eng, ev, need)
        for (nm, lo, hi) in ww:
            for (l2, h2, ev) in self.Wr.get(nm, ()):
                if l2 <= hi and lo <= h2:
                    self._need(eng, ev, need)
            for (l2, h2, ev) in self.Rd.get(nm, ()):
                if l2 <= hi and lo <= h2:
                    self._need(eng, ev, need)
        e = getattr(self.nc, eng)
        wd = self.waited[eng]
        for sid, (sem, val) in need.items():
            if wd.get(sid, 0) < val:
                e.wait_ge(sem, val)
                wd[sid] = val
        self.pending = (rr, ww)

    def _record(self, ev):
        rr, ww = self.pending
        for (nm, lo, hi) in ww:
            wl = [x for x in self.Wr.get(nm, []) if not (lo <= x[0] and x[1] <= hi)]
            wl.append((lo, hi, ev))
            self.Wr[nm] = wl
            self.Rd[nm] = [x for x in self.Rd.get(nm, []) if not (lo <= x[0] and x[1] <= hi)]
        for (nm, lo, hi) in rr:
            if nm in self.readonly:
                continue
            rl = [x for x in self.Rd.get(nm, []) if not (x[0] == lo and x[1] == hi and x[2][0][0] == ev[0][0])]
            rl.append((lo, hi, ev))
            self.Rd[nm] = rl
        self.pending = None
        self.last = ev

    def emit(self, eng, fn, dma=False, chain=False):
        if dma:
            return self._emit_dma(eng, fn)
        if eng not in self.sem or self.cnt[eng] >= 30000:
            self.sem[eng] = self._newsem()
            self.cnt[eng] = 0
        ins = fn(_Proxy(self, eng))
        self.cnt[eng] += 1
        ins.then_inc(self.sem[eng][1], 1)
        self._record((self.sem[eng], self.cnt[eng], eng))

    def _emit_dma(self, q, fn):
        if q not in self.dpool:
            self.dpool[q] = [self._newsem() for _ in range(self.NDMA)]
            self.dcnt[q] = [0] * self.NDMA
            self.dk[q] = 0
        slot = self.dk[q] % self.NDMA
        self.dk[q] += 1
        if self.dcnt[q][slot] >= 30000:
            self.dpool[q][slot] = self._newsem()
            self.dcnt[q][slot] = 0
        sid, sem = self.dpool[q][slot]
        if self.dcnt[q][slot] > 0 and self.waited[q].get(sid, 0) < self.dcnt[q][slot]:
            getattr(self.nc, q).wait_ge(sem, self.dcnt[q][slot])
            self.waited[q][sid] = self.dcnt[q][slot]
        ins = fn(_Proxy(self, q))
        self.dcnt[q][slot] += 16
        ins.then_inc(sem, 16)
        self._record(((sid, sem), self.dcnt[q][slot], "dma"))

    def dma(self, out, in_, q="sync", **kw):
        self.emit(q, lambda e: e.dma_start(out=out, in_=in_, **kw), dma=True)

    def mm(self, out, lhsT, rhs, start=True, stop=True):
        self.emit("tensor", lambda e: e.matmul(out, lhsT, rhs, start=start, stop=stop))

    def act(self, out, in_, func, **kw):
        self.emit("scalar", lambda e: e.activation(out=out, in_=in_, func=func, **kw))

    def V(self, fn):
        self.emit("vector", fn)

    def G(self, fn):
        self.emit("gpsimd", fn)

    def barrier(self):
        evs = [(sid, sem, self.cnt[eng]) for eng, (sid, sem) in self.sem.items()]
        for q in self.dpool:
            evs += [(sid, sem, c) for (sid, sem), c in zip(self.dpool[q], self.dcnt[q]) if c]
        for eng in self.ENG:
            e = getattr(self.nc, eng)
            wd = self.waited[eng]
            for sid, sem, val in evs:
                if wd.get(sid, 0) < val:
                    e.wait_ge(sem, val)
                    wd[sid] = val

    def phase(self):
        prog = self

        class _Ph(ExitStack):
            def __exit__(self, *a):
                prog.barrier()
                return super().__exit__(*a)
        return _Ph()

    def finish(self):
        e = self.nc.sync
        for eng, (sid, sem) in self.sem.items():
            e.wait_ge(sem, self.cnt[eng])
        for q in self.dpool:
            for (sid, sem), c in zip(self.dpool[q], self.dcnt[q]):
                if c:
                    e.wait_ge(sem, c)


def bc(ap, axis, n):
    a = ap.unsqueeze(axis)
    shp = list(a.shape)
    shp[axis] = n
    return a.broadcast_to(shp)


def build(layers, n_ctx_blk, n_lat_blk, off, R, gather, final_norm=True, n_cores=8):
    NB = n_ctx_blk + n_lat_blk
    NT = NB * 128
    nc = bass.Bass("TRN2", target_bir_lowering=False)
    x_in = nc.dram_tensor("x_in", [NT, D], F32, kind="ExternalInput").ap()
    cvec = nc.dram_tensor("cvec", [2, D], F32, kind="ExternalInput").ap()
    y_out = nc.dram_tensor("y_out", [n_lat_blk * 128, D], F32, kind="ExternalOutput").ap()
    if gather:
        wsh = nc.dram_tensor("wsh", [R // n_cores, D], F32, kind="ExternalInput").ap()
        wsrc = nc.dram_tensor("wsrc", [R // n_cores, D], F32).ap()
        W = nc.dram_tensor("wall", [R, D], F32).ap()
    else:
        W = nc.dram_tensor("wall", [R, D], F32, kind="ExternalInput").ap()
    X = nc.dram_tensor("X", [NT, D], F32).ap()
    ub16 = nc.dram_tensor("ub16", [16384, D], BF16).ap()
    vb16 = nc.dram_tensor("vb16", [16384, D], BF16).ap()
    S_d = nc.dram_tensor("S_d", [NT, 2048], F32).ap()
    ST_d = nc.dram_tensor("ST_d", [NT, 16], F32).ap()
    HT_d = nc.dram_tensor("HT_d", [NB * 128, D], BF16).ap()
    has_ssm = any(i % 2 == 1 for i in layers)
    if has_ssm:
        UT_d = [nc.dram_tensor(f"UT_d{d}", [NB * 128, D], BF16).ap() for d in range(2)]
        U_d = nc.dram_tensor("U_d", [NT, D], F32).ap()
        Y_d = [nc.dram_tensor(f"Y_d{d}", [NT, D], F32).ap() for d in range(2)]

    with ExitStack() as top:
        P = Prog(nc, top, readonly=("wall", "x_in", "cvec"))
        _ctr = [0]

        def sb(name, shape, dt, st=top):
            _ctr[0] += 1
            return st.enter_context(nc.sbuf_tensor(f"{name}_{_ctr[0]}", shape, dt))
        ps = [top.enter_context(nc.psum_tensor(f"psum{j}", [128, 1024], F32)) for j in range(4)]

        ident = sb("ident", [128, 128], F32)
        jrev = sb("jrev", [128, 128], F32)
        identb = sb("identb", [128, 128], BF16)
        ones = sb("ones", [128, 128], F32)
        P.G(lambda e: e.memset(ones[:], 1.0))
        P.G(lambda e: e.affine_select(out=ident[:], in_=ones[:], pattern=[[1, 128]], compare_op=ALU.is_equal,
                                      fill=0.0, base=0, channel_multiplier=-1))
        P.G(lambda e: e.affine_select(out=jrev[:], in_=ones[:], pattern=[[1, 128]], compare_op=ALU.is_equal,
                                      fill=0.0, base=-127, channel_multiplier=1))
        P.V(lambda e: e.tensor_copy(out=identb[:], in_=ident[:]))

        if gather:
            P.dma(wsrc[:, :], wsh[:, :])
            P.emit("gpsimd", lambda e: e.collective_compute(
                "AllGather", ALU.bypass, ins=[wsrc[:, :]], outs=[W[:, :]],
                replica_groups=[list(range(n_cores))]), dma=True)

        def is_ctx(b):
            return b < n_ctx_blk

        xt = sb("xt", [128, D], F32)
        xs = sb("xs", [128, D], F32)
        tmp = sb("tmp", [128, D], F32)
        hT = sb("hT", [128, 8, 128], BF16)
        st1 = sb("st1", [128, 8], F32)
        scT = sb("scT", [128, 8, 2], F32)
        modT = sb("modT", [128, 48, 2], F32)
        gB = sb("gB", [128, 2, 2, D], F32)
        gwT = sb("gwT", [128, 2, 8, 2], F32)

        for v in range(2):
            P.dma(scT[:, :, v], cvec[v:v + 1, :].rearrange("o (k p) -> p (o k)", p=128), allow_slow_non_contiguous=True)
        P.act(scT[:], scT[:], AF.Silu)

        for b in range(NB):
            P.dma(xt[:], x_in[b * 128:(b + 1) * 128, :])
            if not is_ctx(b):
                lb = b - n_ctx_blk
                P.dma(xs[:], W[off["pos"] + lb * 128: off["pos"] + (lb + 1) * 128, :])
                P.V(lambda e: e.tensor_add(out=xt[:], in0=xt[:], in1=xs[:]))
            P.dma(X[b * 128:(b + 1) * 128, :], xt[:])

        def load_cast(dst_bf, src_rows, stg):
            P.dma(stg, src_rows.rearrange("(k p) c -> p k c", p=128))
            P.act(dst_bf, stg, AF.Copy)

        def modulation(i, stg, ls):
            scR = sb("scR", [128, 2, 8, 128], F32, ls)
            ngT = sb("ngT", [128, 2, 8], F32, ls)
            abT = sb("abT", [128, 48], F32, ls)
            abB = sb("abB", [128, 2, D], F32, ls)
            for v in range(2):
                P.V(lambda e, v=v: e.tensor_copy(out=scR[:, v], in_=bc(scT[:, :, v], 2, 128)))
            P.dma(abT[:], W[off[f"ada_b{i}"]:off[f"ada_b{i}"] + 6, :].rearrange("s (k p) -> p (s k)", p=128),
                  allow_slow_non_contiguous=True)
            for w, seg in enumerate((2, 5)):
                P.dma(abB[:, w, :], W[off[f"ada_b{i}"] + seg: off[f"ada_b{i}"] + seg + 1, :].partition_broadcast(128))
            for nn, nm in enumerate((f"n1g{i}", f"n2g{i}")):
                P.dma(ngT[:, nn, :], W[off[nm]:off[nm] + 1, :].rearrange("o (k p) -> p (o k)", p=128),
                      allow_slow_non_contiguous=True)
            for seg in range(6):
                r0 = off[f"ada_w{i}"] + seg * D
                P.dma(stg, W[r0:r0 + D, :].rearrange("(k p) c -> p k c", p=128))
                for kc in range(8):
                    for kd in range(8):
                        P.mm(ps[0][:, (seg * 8 + kc) * 2:(seg * 8 + kc) * 2 + 2], stg[:, kd, kc * 128:(kc + 1) * 128],
                             scT[:, kd, :], start=(kd == 0), stop=(kd == 7))
                if seg in (2, 5):
                    w = 0 if seg == 2 else 1
                    for v in range(2):
                        for n in range(2):
                            for kd in range(8):
                                P.mm(ps[1 + v][:, n * 512:(n + 1) * 512], scR[:, v, kd, :],
                                     stg[:, kd, n * 512:(n + 1) * 512], start=(kd == 0), stop=(kd == 7))
                        P.V(lambda e, v=v, w=w: e.tensor_add(out=gB[:, w, v, :], in0=ps[1 + v][:], in1=abB[:, w, :]))
            P.V(lambda e: e.tensor_add(out=modT[:], in0=ps[0][:, 0:96].rearrange("p (s v) -> p s v", v=2),
                                       in1=bc(abT[:], 2, 2)))
            for nn, sseg in enumerate((1, 4)):
                P.V(lambda e, nn=nn, sseg=sseg: e.scalar_tensor_tensor(
                    out=gwT[:, nn], in0=modT[:, sseg * 8:(sseg + 1) * 8, :], scalar=1.0,
                    in1=bc(ngT[:, nn, :], 2, 2), op0=ALU.add, op1=ALU.mult))

        def norm_T(nn, v, dst, src=None, rev=False):
            src = xt if src is None else src
            sseg = 0 if nn == 0 else 3
            P.act(xs[:], src[:], AF.Square, accum_out=st1[:, 0:1])
            P.act(st1[:, 1:2], st1[:, 0:1], AF.Sqrt, scale=1.0 / D, bias=epsb[:, 0:1])
            P.V(lambda e: e.reciprocal(out=st1[:, 2:3], in_=st1[:, 1:2]))
            P.act(xs[:], src[:], AF.Copy, scale=st1[:, 2:3])
            pt = ps[0][:].rearrange("p (k t) -> p k t", t=128)
            for k in range(8):
                P.mm(pt[:, k, :], xs[:, k * 128:(k + 1) * 128], (jrev if rev else ident)[:])
            for k in range(8):
                P.act(dst[:, k, :], pt[:, k, :], AF.Identity, scale=gwT[:, nn, k, v:v + 1],
                      bias=modT[:, sseg * 8 + k, v:v + 1])

        epsb = sb("epsb", [128, 1], F32)
        P.G(lambda e: e.memset(epsb[:], EPS))

        for li, i in enumerate(layers):
            last = (i == 3)
            with P.phase() as ls:
                stg = sb(f"stg{i}", [128, 8, D], F32, ls)
                modulation(i, stg[:], ls)
            C = SimpleNamespace(**{k: v for k, v in locals().items() if not k.startswith('_')})
            if i % 2 == 0:
                chunk_layer(C, i)
            else:
                ssm_layer(C, i, last)
            C = SimpleNamespace(**{k: v for k, v in locals().items() if not k.startswith('_')})
            peer_layer(C, i, last)

        fg = sb("fg", [128, D], F32)
        P.dma(fg[:], W[off["final_g"]:off["final_g"] + 1, :].partition_broadcast(128))
        for b in range(n_ctx_blk, NB):
            P.dma(xt[:], X[b * 128:(b + 1) * 128, :])
            if final_norm:
                P.act(xs[:], xt[:], AF.Square, accum_out=st1[:, 0:1])
                P.act(st1[:, 1:2], st1[:, 0:1], AF.Sqrt, scale=1.0 / D, bias=epsb[:, 0:1])
                P.V(lambda e: e.reciprocal(out=st1[:, 2:3], in_=st1[:, 1:2]))
                P.V(lambda e: e.scalar_tensor_tensor(out=xs[:], in0=xt[:], scalar=st1[:, 2:3], in1=fg[:],
                                                     op0=ALU.mult, op1=ALU.mult))
                P.dma(y_out[(b - n_ctx_blk) * 128:(b - n_ctx_blk + 1) * 128, :], xs[:])
            else:
                P.dma(y_out[(b - n_ctx_blk) * 128:(b - n_ctx_blk + 1) * 128, :], xt[:])
        P.finish()
    return nc


def chunk_layer(C, i):
    nc, P, sb, ps, off, W, X, NB, is_ctx = C.nc, C.P, C.sb, C.ps, C.off, C.W, C.X, C.NB, C.is_ctx
    xt, xs, tmp, hT, gB, norm_T, load_cast, epsb = C.xt, C.xs, C.tmp, C.hT, C.gB, C.norm_T, C.load_cast, C.epsb
    with P.phase() as cs:
        stg = sb("cstg", [128, 8, D], F32, cs)
        w_in = sb("cw_in", [128, 8, 2048], BF16, cs)
        w_out = sb("cw_out", [128, 8, D], BF16, cs)
        wsT = sb("cwsT", [128, 8, 128], BF16, cs)
        wsTf = sb("cwsTf", [128, 8, 128], F32, cs)
        lng = sb("clng", [128, D], F32, cs)
        lnb = sb("clnb", [128, D], F32, cs)
        bsb = sb("cbsb", [128, D], F32, cs)
        uT = sb("cuT", [128, D], F32, cs)
        vv = sb("cvv", [128, D], F32, cs)
        vln = sb("cvln", [128, D], BF16, cs)
        mT = sb("cmT", [128, 8, 128], BF16, cs)
        bst = sb("cbst", [128, 2, 6], F32, cs)
        mv = sb("cmv", [128, 4], F32, cs)
        o = off[f"cm_w_in{i}"]
        for m in range(2):
            load_cast(w_in[:, :, m * D:(m + 1) * D], W[o + m * D:o + (m + 1) * D, :], stg[:])
        o = off[f"cm_w_out{i}"]
        load_cast(w_out[:], W[o:o + D, :], stg[:])
        o = off[f"cm_wsT{i}"]
        P.dma(wsTf[:], W[o:o + 128, :].rearrange("r (a p) -> (r a) p", p=128).rearrange("(h q) p -> q h p", q=128))
        P.V(lambda e: e.tensor_copy(out=wsT[:], in_=wsTf[:]))
        for t_, nm in ((lng, "cm_ln_g"), (lnb, "cm_ln_b"), (bsb, "cm_bs")):
            o = off[f"{nm}{i}"]
            P.dma(t_[:], W[o:o + 1, :].partition_broadcast(128))
        for b in range(NB):
            v = 1 if is_ctx(b) else 0
            P.dma(xt[:], X[b * 128:(b + 1) * 128, :])
            norm_T(0, v, hT)
            pu = ps[1][:].rearrange("p (f t) -> p f t", t=128)
            for f in range(8):
                for kd in range(8):
                    P.mm(pu[:, f, :], w_in[:, kd, f * 128:(f + 1) * 128], hT[:, kd, :], start=(kd == 0), stop=(kd == 7))
            for n in range(2):
                P.act(uT[:, n * 512:(n + 1) * 512], ps[1][:, n * 512:(n + 1) * 512], AF.Gelu)
            for n in range(2):
                for kd in range(8):
                    P.mm(ps[2][:, n * 512:(n + 1) * 512], hT[:, kd, :], w_in[:, kd, D + n * 512:D + (n + 1) * 512],
                         start=(kd == 0), stop=(kd == 7))
            for n in range(2):
                P.act(vv[:, n * 512:(n + 1) * 512], ps[2][:, n * 512:(n + 1) * 512], AF.Gelu)
            for n in range(2):
                P.V(lambda e, n=n: e.bn_stats(out=bst[:, n, :], in_=vv[:, n * 512:(n + 1) * 512]))
            P.V(lambda e: e.bn_aggr(out=mv[:, 0:2], in_=bst[:]))
            P.act(mv[:, 2:3], mv[:, 1:2], AF.Sqrt, scale=1.0, bias=epsb[:, 0:1])
            P.V(lambda e: e.reciprocal(out=mv[:, 3:4], in_=mv[:, 2:3]))
            P.V(lambda e: e.tensor_scalar(out=vv[:], in0=vv[:], scalar1=mv[:, 0:1], scalar2=mv[:, 3:4],
                                          op0=ALU.subtract, op1=ALU.mult))
            P.V(lambda e: e.tensor_mul(out=vv[:], in0=vv[:], in1=lng[:]))
            P.V(lambda e: e.tensor_add(out=vln[:], in0=vv[:], in1=lnb[:]))
            pss = ps[3][:].rearrange("p (h t) -> p h t", t=128)
            for h in range(8):
                P.mm(pss[:, h, :], vln[:, h * 128:(h + 1) * 128], wsT[:, h, :])
            P.V(lambda e: e.tensor_add(out=tmp[:], in0=ps[3][:], in1=bsb[:]))
            P.V(lambda e: e.tensor_mul(out=mT[:].rearrange("p h t -> p (h t)"), in0=tmp[:], in1=uT[:]))
            for n in range(2):
                for h in range(8):
                    P.mm(ps[1][:, n * 512:(n + 1) * 512], mT[:, h, :], w_out[:, h, n * 512:(n + 1) * 512],
                         start=(h == 0), stop=(h == 7))
            P.V(lambda e, v=v: e.tensor_mul(out=tmp[:], in0=ps[1][:], in1=gB[:, 0, v, :]))
            P.V(lambda e: e.tensor_add(out=xt[:], in0=xt[:], in1=tmp[:]))
            P.dma(X[b * 128:(b + 1) * 128, :], xt[:])


def peer_layer(C, i, last):
    nc, P, sb, ps, off, W, X, NB, is_ctx = C.nc, C.P, C.sb, C.ps, C.off, C.W, C.X, C.NB, C.is_ctx
    xt, xs, tmp, hT, gB, norm_T, load_cast = C.xt, C.xs, C.tmp, C.hT, C.gB, C.norm_T, C.load_cast
    ub16, vb16, S_d, ST_d, HT_d, identb = C.ub16, C.vb16, C.S_d, C.ST_d, C.HT_d, C.identb
    blocks = [b for b in range(NB) if not (last and is_ctx(b))]
    with P.phase() as c0:
        stgs = [sb("pstg", [128, 8, D], F32, c0) for _ in range(2)]
        stbs = [sb("pstb", [128, 8, D], BF16, c0) for _ in range(2)]
        k_ = 0
        for name, dst in ((f"uT{i}", ub16), (f"v{i}", vb16)):
            for sc in range(16):
                o = off[name] + sc * D
                P.dma(stgs[k_ % 2][:], W[o:o + D, :].rearrange("(k p) c -> p k c", p=128))
                P.act(stbs[k_ % 2][:], stgs[k_ % 2][:], AF.Copy)
                P.dma(dst[sc * D:(sc + 1) * D, :].rearrange("(k p) c -> p k c", p=128), stbs[k_ % 2][:])
                k_ += 1
    with P.phase() as cs:
        Weff = sb("pWeff", [128, 8, 2048], BF16, cs)
        with P.phase() as c1:
            stg = sb("pstg1", [128, 8, D], F32, c1)
            wqT = sb("pwqT", [128, 32, D], BF16, c1)
            keysT = sb("pkeysT", [128, 32, 128], BF16, c1)
            for m in range(4):
                o = off[f"wqT{i}"] + m * D
                load_cast(wqT[:, m * 8:(m + 1) * 8, :], W[o:o + D, :], stg[:])
            o = off[f"keysT{i}"]
            kf = stg[:].rearrange("p k c -> p (k c)")[:, 0:4096].rearrange("p (a m) -> p a m", m=128)
            P.dma(kf, W[o:o + 512, :].rearrange("r (a m) -> (r a) m", m=128).rearrange("(a e) m -> e a m", e=128))
            P.act(keysT[:], kf, AF.Copy)
            for dk in range(8):
                for hq in range(4):
                    bank = ps[1 + (hq % 2)][:, 0:512]
                    pw = bank.rearrange("p (a m) -> p a m", m=128)
                    for a_ in range(4):
                        hk = hq * 4 + a_
                        for j in range(2):
                            P.mm(pw[:, a_, :], wqT[:, hk * 2 + j, dk * 128:(dk + 1) * 128], keysT[:, hk * 2 + j, :],
                                 start=(j == 0), stop=(j == 1))
                    P.act(Weff[:, dk, hq * 512:(hq + 1) * 512], bank, AF.Copy)
        ss = [sb("ps_", [128, 16, 128], F32, cs) for _ in range(2)]
        hTs = [sb("phT1", [128, 8, 128], BF16, cs) for _ in range(2)]
        s2 = sb("ps2", [128, 128], F32, cs)
        T16 = sb("pT16", [128, 16, 16], F32, cs)
        cand = sb("pcand", [128, 8, 256], F32, cs)
        cand2 = sb("pcand2", [128, 256], F32, cs)
        TS = sb("pTS", [128, 8, 16], F32, cs)
        ex = sb("pex", [128, 8, 16], F32, cs)
        stat = sb("pstat", [128, 16], F32, cs)
        zz = sb("pzz", [128, 8], F32, cs)

        def front(bi):
            b = blocks[bi]
            hT_, s = hTs[bi % 2], ss[bi % 2]
            v = 1 if is_ctx(b) else 0
            P.dma(xt[:], X[b * 128:(b + 1) * 128, :])
            norm_T(1, v, hT_)
            P.dma(HT_d[b * 128:(b + 1) * 128, :].rearrange("p (k t) -> p k t", t=128), hT_[:])
            for half in range(2):
                for n in range(2):
                    col = half * 1024 + n * 512
                    for kd in range(8):
                        P.mm(ps[1 + half][:, n * 512:(n + 1) * 512], hT_[:, kd, :], Weff[:, kd, col:col + 512],
                             start=(kd == 0), stop=(kd == 7))
                    P.act(s[:, half * 8 + n * 4:half * 8 + n * 4 + 4, :],
                          ps[1 + half][:, n * 512:(n + 1) * 512].rearrange("p (a m) -> p a m", m=128), AF.Copy)
            P.dma(S_d[b * 128:(b + 1) * 128, :], s[:].rearrange("p a m -> p (a m)"))

        def back(bi):
            b = blocks[bi]
            s = ss[bi % 2]
            for hk in range(16):
                P.V(lambda e, hk=hk: e.max(out=T16[:, hk, 0:8], in_=s[:, hk, :]))
                P.V(lambda e, hk=hk: e.match_replace(out=s2[:], in_to_replace=T16[:, hk, 0:8], in_values=s[:, hk, :],
                                                     imm_value=NEG))
                P.V(lambda e, hk=hk: e.max(out=T16[:, hk, 8:16], in_=s2[:]))
            T4 = T16[:].rearrange("p (h k) a -> p h k a", k=2)
            P.V(lambda e: e.tensor_tensor(out=cand[:].rearrange("p h (a b) -> p h a b", b=16),
                                          in0=bc(T4[:, :, 0, :], 3, 16), in1=bc(T4[:, :, 1, :], 2, 16), op=ALU.add))
            for h in range(8):
                P.V(lambda e, h=h: e.max(out=TS[:, h, 0:8], in_=cand[:, h, :]))
                P.V(lambda e, h=h: e.match_replace(out=cand2[:], in_to_replace=TS[:, h, 0:8], in_values=cand[:, h, :],
                                                   imm_value=NEG))
                P.V(lambda e, h=h: e.max(out=TS[:, h, 8:16], in_=cand2[:]))
            P.V(lambda e: e.tensor_tensor(out=ex[:], in0=TS[:], in1=bc(TS[:, :, 0], 2, 16), op=ALU.subtract))
            P.act(ex[:], ex[:], AF.Exp)
            P.V(lambda e: e.tensor_reduce(out=zz[:], in_=ex[:], axis=AX.X, op=ALU.add))
            P.act(zz[:], zz[:], AF.Ln)
            P.V(lambda e: e.tensor_copy(out=stat[:, 0:8], in_=TS[:, :, 15]))
            P.V(lambda e: e.scalar_tensor_tensor(out=stat[:, 8:16], in0=TS[:, :, 0], scalar=-1.0, in1=zz[:],
                                                 op0=ALU.mult, op1=ALU.subtract))
            P.dma(ST_d[b * 128:(b + 1) * 128, :], stat[:])

        front(0)
        for bi in range(len(blocks)):
            if bi + 1 < len(blocks):
                front(bi + 1)
            back(bi)
    DELTA = 1e-5
    GBK = 2
    HP = 7
    with P.phase() as cs:
        UTc = [sb("pUTc", [128, 8, 512], BF16, cs) for _ in range(3)]
        Vc = [sb("pVc", [128, 4, D], BF16, cs) for _ in range(3)]
        sS = [sb("psS", [128, 16, 128], F32, cs) for _ in range(GBK)]
        statb = [sb("pstatb", [128, 16], F32, cs) for _ in range(GBK)]
        cst = [sb("pcst", [128, 8], F32, cs) for _ in range(GBK)]
        hTg = [sb("phTg", [128, 8, 128], BF16, cs) for _ in range(GBK)]
        rr_ = [sb("prr", [128, 8, 512], F32, cs) for _ in range(2)]
        wv = [sb("pwv", [128, 8, 512], BF16, cs) for _ in range(2)]
        mk = [sb("pmk", [128, 8, 512], BF16, cs) for _ in range(2)]
        t4 = sb("pt4", [128, 4, 512], BF16, cs)
        t2 = sb("pt2", [128, 2, 512], BF16, cs)
        Gs = [sb("pGs", [128, 512], BF16, cs) for _ in range(2)]
        ad = [sb("pad", [128, 512], F32, cs) for _ in range(2)]
        A = [sb("pA", [128, 512], BF16, cs) for _ in range(2)]
        AT = [sb("pAT", [128, 4, 128], BF16, cs) for _ in range(2)]
        psd = [ps[0][:, 0:512], ps[0][:, 512:1024]]
        psT = [ps[1][:, 0:512], ps[1][:, 512:1024]]
        pso = [ps[2], ps[3]]
        xg = [xt, xs]
        groups = [blocks[k:k + GBK] for k in range(0, len(blocks), GBK)]
        for grp in groups:
            ng = len(grp)
            for gi, b in enumerate(grp):
                P.dma(hTg[gi][:], HT_d[b * 128:(b + 1) * 128, :].rearrange("p (k t) -> p k t", t=128))
                P.dma(sS[gi][:].rearrange("p a m -> p (a m)"), S_d[b * 128:(b + 1) * 128, :])
                P.dma(statb[gi][:], ST_d[b * 128:(b + 1) * 128, :])
                s4 = sS[gi][:].rearrange("p (h k) m -> p h k m", k=2)
                P.V(lambda e, gi=gi, s4=s4: e.tensor_tensor(out=s4[:, :, 0, :], in0=s4[:, :, 0, :],
                                                            in1=bc(statb[gi][:, 0:8], 2, 128), op=ALU.subtract))
                P.V(lambda e, s4=s4: e.tensor_scalar_add(out=s4[:, :, 0, :], in0=s4[:, :, 0, :], scalar1=DELTA))
                P.V(lambda e, gi=gi: e.tensor_tensor(out=cst[gi][:], in0=statb[gi][:, 0:8], in1=statb[gi][:, 8:16],
                                                     op=ALU.add))
                P.V(lambda e, gi=gi: e.tensor_scalar_add(out=cst[gi][:], in0=cst[gi][:], scalar1=-DELTA))
            chunks = [(nch, gi) for nch in range(32) for gi in range(ng)]
            N = len(chunks)

            def T(nch):
                sc, half = nch // 2, nch % 2
                P.dma(UTc[nch % 3][:], ub16[sc * D:(sc + 1) * D, half * 512:(half + 1) * 512].rearrange(
                    "(k p) c -> p k c", p=128))
                r0 = nch * 512
                P.dma(Vc[nch % 3][:], vb16[r0:r0 + 512, :].rearrange("(c j) d -> j c d", j=128))

            def R(c):
                nch, gi = chunks[c]
                p = c % 2
                i0 = nch * 4
                s4 = sS[gi][:].rearrange("p (h k) m -> p h k m", k=2)
                rv = rr_[p][:].rearrange("p h (i j) -> p h i j", j=128)
                P.V(lambda e: e.tensor_tensor(out=rv, in0=bc(s4[:, :, 0, i0:i0 + 4], 3, 128),
                                              in1=bc(s4[:, :, 1, :], 2, 4), op=ALU.add))

            def Wst(c):
                nch, gi = chunks[c]
                p = c % 2
                for h in range(8):
                    P.act(wv[p][:, h, :], rr_[p][:, h, :], AF.Exp, bias=cst[gi][:, h:h + 1], scale=1.0)

            def M(c):
                p = c % 2
                P.V(lambda e: e.scalar_tensor_tensor(out=mk[p][:].rearrange("p h x -> p (h x)"),
                                                     in0=rr_[p][:].rearrange("p h x -> p (h x)"), scalar=0.0,
                                                     in1=wv[p][:].rearrange("p h x -> p (h x)"),
                                                     op0=ALU.is_ge, op1=ALU.mult))
                P.V(lambda e: e.tensor_add(out=t4[:], in0=mk[p][:, 0:4, :], in1=mk[p][:, 4:8, :]))
                P.V(lambda e: e.tensor_add(out=t2[:], in0=t4[:, 0:2, :], in1=t4[:, 2:4, :]))
                P.V(lambda e: e.tensor_add(out=Gs[p][:], in0=t2[:, 0, :], in1=t2[:, 1, :]))

            def S1(c):
                nch, gi = chunks[c]
                p = c % 2
                for kd in range(8):
                    P.mm(psd[p], hTg[gi][:, kd, :], UTc[nch % 3][:, kd, :], start=(kd == 0), stop=(kd == 7))

            def S2(c):
                p = c % 2
                P.act(ad[p][:], psd[p], AF.Gelu)

            def S3(c):
                p = c % 2
                P.V(lambda e: e.tensor_mul(out=A[p][:], in0=ad[p][:], in1=Gs[p][:]))

            def S456(c):
                nch, gi = chunks[c]
                p = c % 2
                pT = psT[p].rearrange("p (c t) -> p c t", t=128)
                for cc in range(4):
                    P.mm(pT[:, cc, :], A[p][:, cc * 128:(cc + 1) * 128], identb[:])
                P.act(AT[p][:], pT, AF.Copy)
                for cc in range(4):
                    for nn in range(2):
                        P.mm(pso[gi][:, nn * 512:(nn + 1) * 512], AT[p][:, cc, :],
                             Vc[nch % 3][:, cc, nn * 512:(nn + 1) * 512],
                             start=(nch == 0 and cc == 0), stop=(nch == 31 and cc == 3))

            T(0)
            T(1)
            R(0)
            Wst(0)
            S1(0)
            S2(0)
            for c in range(N):
                nch, gi = chunks[c]
                if gi == 0 and nch + 2 < 32:
                    T(nch + 2)
                if c + 1 < N:
                    R(c + 1)
                    Wst(c + 1)
                    S1(c + 1)
                    S2(c + 1)
                M(c)
                S3(c)
                S456(c)
            for gi, b in enumerate(grp):
                v = 1 if is_ctx(b) else 0
                P.dma(xg[gi][:], X[b * 128:(b + 1) * 128, :])
                P.V(lambda e, gi=gi, v=v: e.tensor_mul(out=tmp[:], in0=pso[gi][:], in1=gB[:, 1, v, :]))
                P.V(lambda e, gi=gi: e.tensor_add(out=xg[gi][:], in0=xg[gi][:], in1=tmp[:]))
                P.dma(X[b * 128:(b + 1) * 128, :], xg[gi][:])


def ssm_layer(C, i, last):
    nc, P, sb, ps, off, W, X, NB, is_ctx = C.nc, C.P, C.sb, C.ps, C.off, C.W, C.X, C.NB, C.is_ctx
    xt, xs, tmp, hT, gB, norm_T, load_cast = C.xt, C.xs, C.tmp, C.hT, C.gB, C.norm_T, C.load_cast
    UT_d, U_d, Y_d, ident, jrev, n_ctx_blk = C.UT_d, C.U_d, C.Y_d, C.ident, C.jrev, C.n_ctx_blk

    def mirror(b):
        return (n_ctx_blk - 1 - b) if is_ctx(b) else (n_ctx_blk + (NB - 1 - b))

    with P.phase() as cs:
        stg = sb("sstg", [128, 8, D], F32, cs)
        w_in = sb("sw_in", [128, 8, D], BF16, cs)
        hTr = sb("shTr", [128, 8, 128], BF16, cs)
        uTb = sb("suTb", [128, 8, 128], BF16, cs)
        o = off[f"ssm_w_in{i}"]
        load_cast(w_in[:], W[o:o + D, :], stg[:])
        for b in range(NB):
            v = 1 if is_ctx(b) else 0
            P.dma(xt[:], X[b * 128:(b + 1) * 128, :])
            norm_T(0, v, hT)
            norm_T(0, v, hTr, rev=True)
            for n in range(2):
                for kd in range(8):
                    P.mm(ps[1][:, n * 512:(n + 1) * 512], hT[:, kd, :], w_in[:, kd, n * 512:(n + 1) * 512],
                         start=(kd == 0), stop=(kd == 7))
            P.V(lambda e: e.tensor_copy(out=tmp[:], in_=ps[1][:]))
            P.dma(U_d[b * 128:(b + 1) * 128, :], tmp[:])
            for d, hsrc, pos in ((0, hT, b), (1, hTr, mirror(b))):
                pu = ps[2][:].rearrange("p (f t) -> p f t", t=128)
                for f in range(8):
                    for kd in range(8):
                        P.mm(pu[:, f, :], w_in[:, kd, f * 128:(f + 1) * 128], hsrc[:, kd, :], start=(kd == 0), stop=(kd == 7))
                P.V(lambda e, pu=pu: e.tensor_copy(out=uTb[:], in_=pu))
                P.dma(UT_d[d][pos * 128:(pos + 1) * 128, :].rearrange("p (k t) -> p k t", t=128), uTb[:])

    with P.phase() as cs:
        Wr = sb("sWr", [128, 32, 128], F32, cs)
        Wi = sb("sWi", [128, 32, 128], F32, cs)
        Rr = sb("sRr", [128, 32, 128], F32, cs)
        Ri = sb("sRi", [128, 32, 128], F32, cs)
        Rho0 = sb("sRho0", [128, 32, 128], F32, cs)
        BTr = sb("sBTr", [128, 8, 4, 128], BF16, cs)
        BTi = sb("sBTi", [128, 8, 4, 128], BF16, cs)
        CTr = sb("sCTr", [128, 32, 32], BF16, cs)
        CTi = sb("sCTi", [128, 32, 32], BF16, cs)
        big = sb("sbig", [128, 6, 1024], F32, cs)
        wk = [big[:, j, :] for j in range(6)]
        hb = [sb(f"shb{j}", [128, 1024], BF16, cs) for j in range(2)]
        uTs = [sb("suT", [128, 8, 128], BF16, cs) for _ in range(2)]
        sm = [sb(f"ssm{j}", [128, 32], F32, cs) for j in range(12)]
        carS = [sb(f"scar{j}", [128, 32], F32, cs) for j in range(2)]
        ti32 = sb("sti32", [128, 32], mybir.dt.int32, cs)
        lr, li_, dl, rho, th, cs_, sn_, cr, ci, t0, t1_, t2_ = sm
        for d in range(2):
            with P.phase() as ss:
                stg = big[:].rearrange("p a c -> p (a c)")[:, 0:4096]
                for name, dst in ((f"bT_r{i}", BTr), (f"bT_i{i}", BTi)):
                    o = off[name] + d * 512
                    P.dma(stg[:], W[o:o + 512, :].rearrange("(p r) c -> p (r c)", p=128))
                    P.act(dst[:].rearrange("p a b c -> p (a b c)"), stg[:], AF.Copy)
                for name, dst, scl in ((f"cT_r{i}", CTr, 1.0), (f"cT_i{i}", CTi, -1.0)):
                    o = off[name] + d * 128
                    P.dma(stg[:, 0:1024], W[o:o + 128, :])
                    P.act(dst[:].rearrange("p a b -> p (a b)"), stg[:, 0:1024], AF.Copy, scale=scl)
                for name, dst in ((f"lamre{i}", lr), (f"lamim{i}", li_), (f"lstep{i}", dl)):
                    o = off[name] + d * 4
                    P.dma(dst[:], W[o:o + 4, :].rearrange("r (a g) -> (r a) g", g=32))
                P.act(dl[:], dl[:], AF.Exp)
                P.V(lambda e: e.tensor_mul(out=rho[:], in0=lr[:], in1=dl[:]))
                P.act(rho[:], rho[:], AF.Exp)
                P.V(lambda e: e.tensor_mul(out=th[:], in0=li_[:], in1=dl[:]))
                for dst, shift in ((sn_, 0.0), (cs_, 0.5 * PI)):
                    P.V(lambda e, shift=shift: e.tensor_scalar_add(out=t1_[:], in0=th[:], scalar1=shift))
                    P.V(lambda e: e.tensor_scalar_mul(out=ti32[:], in0=t1_[:], scalar1=1.0 / (2.0 * PI)))
                    P.V(lambda e: e.tensor_copy(out=t0[:], in_=ti32[:]))
                    P.V(lambda e: e.scalar_tensor_tensor(out=t1_[:], in0=t0[:], scalar=-2.0 * PI, in1=t1_[:],
                                                         op0=ALU.mult, op1=ALU.add))
                    P.V(lambda e: e.tensor_scalar(out=t0[:], in0=t1_[:], scalar1=PI, scalar2=-2.0 * PI,
                                                  op0=ALU.is_gt, op1=ALU.mult))
                    P.V(lambda e: e.tensor_add(out=t1_[:], in0=t1_[:], in1=t0[:]))
                    P.V(lambda e: e.tensor_scalar(out=t0[:], in0=t1_[:], scalar1=-PI, scalar2=2.0 * PI,
                                                  op0=ALU.is_lt, op1=ALU.mult))
                    P.V(lambda e: e.tensor_add(out=t1_[:], in0=t1_[:], in1=t0[:]))
                    P.act(dst[:], t1_[:], AF.Sin)
                P.V(lambda e: e.tensor_mul(out=t0[:], in0=rho[:], in1=cs_[:]))
                P.V(lambda e: e.tensor_scalar_add(out=t0[:], in0=t0[:], scalar1=-1.0))
                P.V(lambda e: e.tensor_mul(out=t1_[:], in0=rho[:], in1=sn_[:]))
                P.V(lambda e: e.tensor_mul(out=cr[:], in0=t0[:], in1=lr[:]))
                P.V(lambda e: e.tensor_mul(out=t2_[:], in0=t1_[:], in1=li_[:]))
                P.V(lambda e: e.tensor_add(out=cr[:], in0=cr[:], in1=t2_[:]))
                P.V(lambda e: e.tensor_mul(out=ci[:], in0=t1_[:], in1=lr[:]))
                P.V(lambda e: e.tensor_mul(out=t2_[:], in0=t0[:], in1=li_[:]))
                P.V(lambda e: e.tensor_sub(out=ci[:], in0=ci[:], in1=t2_[:]))
                P.V(lambda e: e.tensor_mul(out=t0[:], in0=lr[:], in1=lr[:]))
                P.V(lambda e: e.tensor_mul(out=t1_[:], in0=li_[:], in1=li_[:]))
                P.V(lambda e: e.tensor_add(out=t0[:], in0=t0[:], in1=t1_[:]))
                P.V(lambda e: e.reciprocal(out=t0[:], in_=t0[:]))
                P.V(lambda e: e.tensor_mul(out=cr[:], in0=cr[:], in1=t0[:]))
                P.V(lambda e: e.tensor_mul(out=ci[:], in0=ci[:], in1=t0[:]))
                A1 = stg[:, 0:2048].rearrange("p (g k) -> p g k", k=64)
                A2 = stg[:, 2048:4096].rearrange("p (g k) -> p g k", k=64)
                P.V(lambda e: e.tensor_copy(out=Wr[:, :, 0], in_=cs_[:]))
                P.V(lambda e: e.tensor_copy(out=Wi[:, :, 0], in_=sn_[:]))
                s_ = 1
                while s_ < 128:
                    mr, mi = bc(Wr[:, :, s_ - 1], 2, s_), bc(Wi[:, :, s_ - 1], 2, s_)
                    a1, a2 = A1[:, :, 0:s_], A2[:, :, 0:s_]
                    P.V(lambda e, a1=a1, mr=mr, s_=s_: e.tensor_mul(out=a1, in0=Wr[:, :, 0:s_], in1=mr))
                    P.V(lambda e, a2=a2, mi=mi, s_=s_: e.tensor_mul(out=a2, in0=Wi[:, :, 0:s_], in1=mi))
                    P.V(lambda e, a1=a1, a2=a2, s_=s_: e.tensor_sub(out=Wr[:, :, s_:2 * s_], in0=a1, in1=a2))
                    P.V(lambda e, a1=a1, mi=mi, s_=s_: e.tensor_mul(out=a1, in0=Wr[:, :, 0:s_], in1=mi))
                    P.V(lambda e, a2=a2, mr=mr, s_=s_: e.tensor_mul(out=a2, in0=Wi[:, :, 0:s_], in1=mr))
                    P.V(lambda e, a1=a1, a2=a2, s_=s_: e.tensor_add(out=Wi[:, :, s_:2 * s_], in0=a1, in1=a2))
                    s_ *= 2
                B1 = stg[:].rearrange("p (g k) -> p g k", k=128)
                P.V(lambda e: e.tensor_mul(out=Rr[:], in0=Wr[:], in1=bc(cr[:], 2, 128)))
                P.V(lambda e: e.tensor_mul(out=B1, in0=Wi[:], in1=bc(ci[:], 2, 128)))
                P.V(lambda e: e.tensor_add(out=Rr[:], in0=Rr[:], in1=B1))
                P.V(lambda e: e.tensor_mul(out=Ri[:], in0=Wr[:], in1=bc(ci[:], 2, 128)))
                P.V(lambda e: e.tensor_mul(out=B1, in0=Wi[:], in1=bc(cr[:], 2, 128)))
                P.V(lambda e: e.tensor_sub(out=Ri[:], in0=Ri[:], in1=B1))
                P.V(lambda e: e.tensor_copy(out=Rho0[:], in_=bc(rho[:], 2, 128)))
                P.V(lambda e: e.memset(Rho0[:, :, 0], 0.0))
                P.V(lambda e: e.memset(carS[0][:], 0.0))
                P.V(lambda e: e.memset(carS[1][:], 0.0))
            steps = [(pos, q) for pos in range(NB) for q in range(4)]

            def load_u(pos):
                P.dma(uTs[pos % 2][:], UT_d[d][pos * 128:(pos + 1) * 128, :].rearrange("p (k t) -> p k t", t=128))

            def inject(pos, q):
                uT = uTs[pos % 2]
                Xr, Xi = ps[0][:], ps[1][:]
                for g8 in range(8):
                    gp = q * 8 + g8
                    ck, g4 = gp // 4, gp % 4
                    P.mm(Xr[:, g8 * 128:(g8 + 1) * 128], BTr[:, ck, g4, :], uT[:, ck, :])
                    P.mm(Xi[:, g8 * 128:(g8 + 1) * 128], BTi[:, ck, g4, :], uT[:, ck, :])

            def rotate_in(pos, q):
                Xr, Xi = ps[0][:], ps[1][:]
                gs = slice(q * 8, (q + 1) * 8)
                Rqr = Rr[:, gs, :].rearrange("p g k -> p (g k)")
                Rqi = Ri[:, gs, :].rearrange("p g k -> p (g k)")
                Wqr = Wr[:, gs, :].rearrange("p g k -> p (g k)")
                Wqi = Wi[:, gs, :].rearrange("p g k -> p (g k)")
                Rhq = Rho0[:, gs, :].rearrange("p g k -> p (g k)")
                k1, k2, btr, bti, gr, gi = wk
                hr, hi = btr, bti
                P.V(lambda e: e.tensor_mul(out=k1, in0=Xr, in1=Rqr))
                P.V(lambda e: e.tensor_mul(out=k2, in0=Xi, in1=Rqi))
                P.V(lambda e: e.tensor_sub(out=btr, in0=k1, in1=k2))
                P.V(lambda e: e.tensor_mul(out=k1, in0=Xi, in1=Rqr))
                P.V(lambda e: e.tensor_mul(out=k2, in0=Xr, in1=Rqi))
                P.V(lambda e: e.tensor_add(out=bti, in0=k1, in1=k2))

            def rest(pos, q):
                gs = slice(q * 8, (q + 1) * 8)
                Wqr = Wr[:, gs, :].rearrange("p g k -> p (g k)")
                Wqi = Wi[:, gs, :].rearrange("p g k -> p (g k)")
                Rhq = Rho0[:, gs, :].rearrange("p g k -> p (g k)")
                k1, k2, btr, bti, gr, gi = wk
                hr, hi = btr, bti
                b3r = btr.rearrange("p (g k) -> p g k", k=128)
                b3i = bti.rearrange("p (g k) -> p g k", k=128)
                P.V(lambda e: e.tensor_add(out=b3r[:, :, 0], in0=b3r[:, :, 0], in1=carS[0][:, gs]))
                P.V(lambda e: e.tensor_add(out=b3i[:, :, 0], in0=b3i[:, :, 0], in1=carS[1][:, gs]))
                P.V(lambda e: e.tensor_tensor_scan(out=gr, data0=Rhq, data1=btr, initial=0.0,
                                                   op0=ALU.mult, op1=ALU.add))
                P.V(lambda e: e.tensor_tensor_scan(out=gi, data0=Rhq, data1=bti, initial=0.0,
                                                   op0=ALU.mult, op1=ALU.add))
                P.V(lambda e: e.tensor_mul(out=k1, in0=gr, in1=Wqr))
                P.V(lambda e: e.tensor_mul(out=k2, in0=gi, in1=Wqi))
                P.V(lambda e: e.tensor_sub(out=hr, in0=k1, in1=k2))
                P.V(lambda e: e.tensor_mul(out=k1, in0=gi, in1=Wqr))
                P.V(lambda e: e.tensor_mul(out=k2, in0=gr, in1=Wqi))
                P.V(lambda e: e.tensor_add(out=hi, in0=k1, in1=k2))
                h3r = hr.rearrange("p (g k) -> p g k", k=128)
                h3i = hi.rearrange("p (g k) -> p g k", k=128)
                P.V(lambda e: e.tensor_mul(out=carS[0][:, gs], in0=h3r[:, :, 127], in1=rho[:, gs]))
                P.V(lambda e: e.tensor_mul(out=carS[1][:, gs], in0=h3i[:, :, 127], in1=rho[:, gs]))
                P.act(hb[0][:], hr, AF.Copy)
                P.act(hb[1][:], hi, AF.Copy)

            def readout(pos, q):
                for g8 in range(8):
                    gp = q * 8 + g8
                    P.mm(ps[2][:, gp * 32:(gp + 1) * 32], hb[0][:, g8 * 128:(g8 + 1) * 128], CTr[:, gp, :],
                         start=True, stop=False)
                    P.mm(ps[2][:, gp * 32:(gp + 1) * 32], hb[1][:, g8 * 128:(g8 + 1) * 128], CTi[:, gp, :],
                         start=False, stop=True)

            load_u(0)
            inject(*steps[0])
            for k, (pos, q) in enumerate(steps):
                if q == 0 and pos + 1 < NB:
                    load_u(pos + 1)
                rotate_in(pos, q)
                if k + 1 < len(steps):
                    inject(*steps[k + 1])
                rest(pos, q)
                readout(pos, q)
                if q == 3:
                    P.V(lambda e: e.tensor_copy(out=tmp[:], in_=ps[2][:]))
                    P.dma(Y_d[d][pos * 128:(pos + 1) * 128, :], tmp[:])

    with P.phase() as cs:
        stg = sb("sstg3", [128, 8, D], F32, cs)
        w_out = sb("sw_out", [128, 8, 2048], BF16, cs)
        dB = sb("sdB", [128, D], F32, cs)
        a1 = sb("sa1", [128, D], F32, cs)
        a2 = sb("sa2", [128, D], F32, cs)
        gyT = sb("sgyT", [128, 8, 128], BF16, cs)
        o = off[f"ssm_w_out{i}"]
        for m in range(2):
            load_cast(w_out[:, :, m * D:(m + 1) * D], W[o + m * D:o + (m + 1) * D, :], stg[:])
        o = off[f"ssm_d{i}"]
        P.dma(dB[:], W[o:o + 1, :].partition_broadcast(128))
        for b in range(NB):
            if last and is_ctx(b):
                continue
            v = 1 if is_ctx(b) else 0
            P.dma(xt[:], X[b * 128:(b + 1) * 128, :])
            P.dma(a1[:], U_d[b * 128:(b + 1) * 128, :])
            P.dma(tmp[:], Y_d[0][b * 128:(b + 1) * 128, :])
            P.dma(a2[:], Y_d[1][mirror(b) * 128:(mirror(b) + 1) * 128, :])
            P.V(lambda e: e.tensor_mul(out=a1[:], in0=a1[:], in1=dB[:]))
            P.V(lambda e: e.tensor_add(out=a1[:], in0=a1[:], in1=tmp[:]))
            pt = ps[0][:].rearrange("p (k t) -> p k t", t=128)
            for k in range(8):
                P.mm(pt[:, k, :], a1[:, k * 128:(k + 1) * 128], ident[:], start=True, stop=False)
                P.mm(pt[:, k, :], a2[:, k * 128:(k + 1) * 128], jrev[:], start=False, stop=True)
            for n in range(2):
                P.act(gyT[:].rearrange("p k t -> p (k t)")[:, n * 512:(n + 1) * 512], ps[0][:, n * 512:(n + 1) * 512], AF.Gelu)
            for n in range(4):
                po = ps[1 + n // 2][:, (n % 2) * 512:(n % 2 + 1) * 512]
                for k in range(8):
                    P.mm(po, gyT[:, k, :], w_out[:, k, n * 512:(n + 1) * 512], start=(k == 0), stop=(k == 7))
            for n in range(2):
                P.act(tmp[:, n * 512:(n + 1) * 512], ps[2][:, n * 512:(n + 1) * 512], AF.Sigmoid)
            P.V(lambda e: e.tensor_mul(out=tmp[:], in0=ps[1][:], in1=tmp[:]))
            P.V(lambda e, v=v: e.tensor_mul(out=tmp[:], in0=tmp[:], in1=gB[:, 0, v, :]))
            P.V(lambda e: e.tensor_add(out=xt[:], in0=xt[:], in1=tmp[:]))
            P.dma(X[b * 128:(b + 1) * 128, :], xt[:])


def run(inp, layers, n_cores, n_ctx, n_lat, final_norm=True, gather=None):
    gather = (n_cores > 1) if gather is None else gather
    Wall, off, R = pack_weights(inp, layers, n_lat)
    nc = build(layers, n_ctx // 128, n_lat // 128, off, R, gather, final_norm, n_cores)
    in_maps = []
    for b in range(n_cores):
        m = {"x_in": np.ascontiguousarray(np.concatenate([inp["ctx"][b, :n_ctx], inp["x"][b, :n_lat]], axis=0), dtype=np.float32),
             "cvec": np.ascontiguousarray(np.stack([inp["c"][b], inp["c_ctx"]]), dtype=np.float32)}
        if gather:
            m["wsh"] = np.ascontiguousarray(Wall[b * (R // n_cores):(b + 1) * (R // n_cores)])
        else:
            m["wall"] = Wall
        in_maps.append(m)
    import os
    if os.environ.get("KTRACE"):
        res = run_bass_kernel_spmd(nc, in_maps, core_ids=list(range(n_cores)), trace=True)
        print("EXEC_TIME_NS", res.exec_time_ns, flush=True)
    else:
        res = run_bass_kernel_spmd(nc, in_maps, core_ids=list(range(n_cores)))
    return np.stack([r["y_out"] for r in res.results], axis=0)


def kernel(**inputs):
    inp = {k: np.asarray(v) for k, v in inputs.items()}
    return run(inp, [0, 1, 2, 3], 8, 256, 8192, gather=False).astype(np.float32)
```

```python
import numpy as np
from contextlib import ExitStack
from types import SimpleNamespace
import concourse.bass as bass
import concourse.mybir as mybir
from concourse.bass_utils import run_bass_kernel_spmd

F32 = mybir.dt.float32
BF16 = mybir.dt.bfloat16
AF = mybir.ActivationFunctionType
ALU = mybir.AluOpType
AX = mybir.AxisListType
D = 1024
NEG = -1.0e30
EPS = 1e-6
PI = float(np.pi)


def _rows(a):
    a = np.ascontiguousarray(a, dtype=np.float32).reshape(-1)
    pad = (-a.size) % D
    if pad:
        a = np.concatenate([a, np.zeros(pad, np.float32)])
    return a.reshape(-1, D)


def _pos_embed(rows):
    def sincos(pos, dim):
        omega = (1.0 / (np.float32(10000.0) ** (np.arange(dim // 2, dtype=np.float32) / np.float32(dim // 2)))).astype(np.float32)
        ang = (pos[:, None] * omega[None, :]).astype(np.float32)
        return np.concatenate([np.sin(ang), np.cos(ang)], axis=-1).astype(np.float32)
    r = np.repeat(np.arange(rows, dtype=np.float32), 64)
    col = np.tile(np.arange(64, dtype=np.float32), rows)
    return np.concatenate([sincos(r, D // 2), sincos(col, D // 2)], axis=-1).astype(np.float32)


def pack_weights(inp, layers, n_lat):
    parts, off = [], {}
    cur = 0

    def add(name, arr):
        nonlocal cur
        r = _rows(arr)
        off[name] = cur
        parts.append(r)
        cur += r.shape[0]

    add("pos", _pos_embed(n_lat // 64))
    add("final_g", inp["final_g"])
    for i in layers:
        k = i // 2
        add(f"ada_w{i}", inp["ada_w"][i].reshape(D, 6, D).transpose(1, 0, 2))
        add(f"ada_b{i}", inp["ada_b"][i])
        add(f"n1g{i}", inp["norm1_g"][i])
        add(f"n2g{i}", inp["norm2_g"][i])
        add(f"wqT{i}", inp["peer_w_q"][i].T)
        add(f"keysT{i}", inp["peer_keys"][i].transpose(0, 1, 3, 2))
        add(f"uT{i}", inp["peer_u"][i].T.reshape(D, 16, D).transpose(1, 0, 2))
        add(f"v{i}", inp["peer_v"][i])
        if i % 2 == 0:
            add(f"cm_w_in{i}", inp["cm_w_in"][k].reshape(D, 2, D).transpose(1, 0, 2))
            add(f"cm_w_out{i}", inp["cm_w_out"][k])
            add(f"cm_ln_g{i}", inp["cm_ln_g"][k])
            add(f"cm_ln_b{i}", inp["cm_ln_b"][k])
            add(f"cm_wsT{i}", inp["cm_w_s"][k].transpose(0, 2, 1))
            add(f"cm_bs{i}", inp["cm_b_s"][k])
        else:
            add(f"ssm_w_in{i}", inp["ssm_w_in"][k])
            add(f"ssm_w_out{i}", inp["ssm_w_out"][k].reshape(D, 2, D).transpose(1, 0, 2))
            def sp(a):
                return a.reshape(2, 32, 128).transpose(0, 2, 1)
            add(f"lamre{i}", sp(inp["ssm_lam_re"][k]))
            add(f"lamim{i}", sp(inp["ssm_lam_im"][k]))
            add(f"lstep{i}", sp(np.broadcast_to(inp["ssm_log_step"][k][:, :, None], (2, 64, 64))))
            def bT(b):
                o = np.zeros((2, 128, 8, 4, 128), np.float32)
                for g in range(64):
                    ch0 = g * 16
                    chunk, cl = ch0 // 128, ch0 % 128
                    o[:, cl:cl + 16, chunk, (g // 2) % 4, (g % 2) * 64:(g % 2) * 64 + 64] = b[:, g].transpose(0, 2, 1)
                return o
            add(f"bT_r{i}", bT(inp["ssm_b_re"][k]))
            add(f"bT_i{i}", bT(inp["ssm_b_im"][k]))
            def cT(c):
                o = np.zeros((2, 128, 32, 32), np.float32)
                for g in range(64):
                    o[:, (g % 2) * 64:(g % 2) * 64 + 64, g // 2, (g % 2) * 16:(g % 2) * 16 + 16] = c[:, g].transpose(0, 2, 1)
                return o
            add(f"cT_r{i}", cT(inp["ssm_c_re"][k]))
            add(f"cT_i{i}", cT(inp["ssm_c_im"][k]))
            add(f"ssm_d{i}", inp["ssm_d"][k])
    pad = (-cur) % 8
    if pad:
        parts.append(np.zeros((pad, D), np.float32))
        cur += pad
    return np.concatenate(parts, axis=0), off, cur


def _is_ap(x):
    return hasattr(x, "ap") and hasattr(x, "offset") and hasattr(x, "space") and hasattr(x, "tensor")


def _region(a):
    steps = list(a.ap)
    sp = str(a.space)
    if sp in ("SB", "PSUM"):
        shape = list(a.tensor.shape)
        row = 1
        for d in shape[1:]:
            row *= d
        if steps and (steps[0][0] == row or steps[0][1] == 1):
            base = a.offset % row
            dims = steps[1:]
        else:
            return (a.name, 0, row - 1)
    else:
        base = a.offset
        dims = steps
    lo = base + sum(min(0, st * (c - 1)) for st, c in dims)
    hi = base + sum(max(0, st * (c - 1)) for st, c in dims)
    if sp == "PSUM":
        lo = (lo // 512) * 512
        hi = (hi // 512) * 512 + 511
    return (a.name, lo, hi)


class _Proxy:
    WRITE_KEYS = ("out", "accum_out", "out_max", "out_indices")

    def __init__(self, prog, eng):
        self.prog, self.eng = prog, eng

    def __getattr__(self, name):
        prog, eng = self.prog, self.eng
        real = getattr(getattr(prog.nc, eng), name)

        def call(*args, **kw):
            reads, writes = [], []
            for j, x in enumerate(args):
                if _is_ap(x):
                    (writes if j == 0 else reads).append(x)
            for k, x in kw.items():
                if _is_ap(x):
                    (writes if k in self.WRITE_KEYS else reads).append(x)
            prog._sync(eng, reads, writes)
            return real(*args, **kw)
        return call


class Prog:
    ENG = ("sync", "scalar", "vector", "gpsimd", "tensor")
    NDMA = 8

    def __init__(self, nc, stack, readonly=()):
        self.nc, self.stack = nc, stack
        self.sem, self.cnt, self.nsem = {}, {}, 0
        self.waited = {e: {} for e in self.ENG}
        self.Wr, self.Rd = {}, {}
        self.readonly = set(readonly)
        self.dpool, self.dcnt, self.dk = {}, {}, {}
        self.pending = None
        self.last = None

    def _newsem(self):
        s = self.stack.enter_context(self.nc.semaphore(f"s{self.nsem}"))
        self.nsem += 1
        return (self.nsem - 1, s)

    def _need(self, eng, ev, need):
        if ev is None:
            return
        (sid, sem), val, peng = ev
        if peng == "tensor" and eng == "tensor":
            return
        if need.get(sid, (None, 0))[1] < val:
            need[sid] = (sem, val)

    def _sync(self, eng, reads, writes):
        need = {}
        rr = [_region(a) for a in reads if str(a.space) != "PSUM"]
        ww = [_region(a) for a in writes] + [_region(a) for a in reads if str(a.space) == "PSUM"]
        for (nm, lo, hi) in rr:
            for (l2, h2, ev) in self.Wr.get(nm, ()):
                if l2 <= hi and lo <= h2:
                    self._need(eng, ev, need)
        for (nm, lo, hi) in ww:
            for (l2, h2, ev) in self.Wr.get(nm, ()):
                if l2 <= hi and lo <= h2:
                    self._need(eng, ev, need)
            for (l2, h2, ev) in self.Rd.get(nm, ()):
                if l2 <= hi and lo <= h2:
                    self._need(eng, ev, need)
        e = getattr(self.nc, eng)
        wd = self.waited[eng]
        for sid, (sem, val) in need.items():
            if wd.get(sid, 0) < val:
                e.wait_ge(sem, val)
                wd[sid] = val
        self.pending = (rr, ww)

    def _record(self, ev):
        rr, ww = self.pending
        for (nm, lo, hi) in ww:
            wl = [x for x in self.Wr.get(nm, []) if not (lo <= x[0] and x[1] <= hi)]
            wl.append((lo, hi, ev))
            self.Wr[nm] = wl
            self.Rd[nm] = [x for x in self.Rd.get(nm, []) if not (lo <= x[0] and x[1] <= hi)]
        for (nm, lo, hi) in rr:
            if nm in self.readonly:
                continue
            rl = [x for x in self.Rd.get(nm, []) if not (x[0] == lo and x[1] == hi and x[2][0][0] == ev[0][0])]
            rl.append((lo, hi, ev))
            self.Rd[nm] = rl
        self.pending = None
        self.last = ev

    def emit(self, eng, fn, dma=False, chain=False):
        if dma:
            return self._emit_dma(eng, fn)
        if eng not in self.sem or self.cnt[eng] >= 30000:
            self.sem[eng] = self._newsem()
            self.cnt[eng] = 0
        ins = fn(_Proxy(self, eng))
        self.cnt[eng] += 1
        ins.then_inc(self.sem[eng][1], 1)
        self._record((self.sem[eng], self.cnt[eng], eng))

    def _emit_dma(self, q, fn):
        if q not in self.dpool:
            self.dpool[q] = [self._newsem() for _ in range(self.NDMA)]
            self.dcnt[q] = [0] * self.NDMA
            self.dk[q] = 0
        slot = self.dk[q] % self.NDMA
        self.dk[q] += 1
        if self.dcnt[q][slot] >= 30000:
            self.dpool[q][slot] = self._newsem()
            self.dcnt[q][slot] = 0
        sid, sem = self.dpool[q][slot]
        if self.dcnt[q][slot] > 0 and self.waited[q].get(sid, 0) < self.dcnt[q][slot]:
            getattr(self.nc, q).wait_ge(sem, self.dcnt[q][slot])
            self.waited[q][sid] = self.dcnt[q][slot]
        ins = fn(_Proxy(self, q))
        self.dcnt[q][slot] += 16
        ins.then_inc(sem, 16)
        self._record(((sid, sem), self.dcnt[q][slot], "dma"))

    def dma(self, out, in_, q="sync", **kw):
        self.emit(q, lambda e: e.dma_start(out=out, in_=in_, **kw), dma=True)

    def mm(self, out, lhsT, rhs, start=True, stop=True):
        self.emit("tensor", lambda e: e.matmul(out, lhsT, rhs, start=start, stop=stop))

    def act(self, out, in_, func, **kw):
        self.emit("scalar", lambda e: e.activation(out=out, in_=in_, func=func, **kw))

    def V(self, fn):
        self.emit("vector", fn)

    def G(self, fn):
        self.emit("gpsimd", fn)

    def barrier(self):
        evs = [(sid, sem, self.cnt[eng]) for eng, (sid, sem) in self.sem.items()]
        for q in self.dpool:
            evs += [(sid, sem, c) for (sid, sem), c in zip(self.dpool[q], self.dcnt[q]) if c]
        for eng in self.ENG:
            e = getattr(self.nc, eng)
            wd = self.waited[eng]
            for sid, sem, val in evs:
                if wd.get(sid, 0) < val:
                    e.wait_ge(sem, val)
                    wd[sid] = val

    def phase(self):
        prog = self

        class _Ph(ExitStack):
            def __exit__(self, *a):
                prog.barrier()
                return super().__exit__(*a)
        return _Ph()

    def finish(self):
        e = self.nc.sync
        for eng, (sid, sem) in self.sem.items():
            e.wait_ge(sem, self.cnt[eng])
        for q in self.dpool:
            for (sid, sem), c in zip(self.dpool[q], self.dcnt[q]):
                if c:
                    e.wait_ge(sem, c)


def bc(ap, axis, n):
    a = ap.unsqueeze(axis)
    shp = list(a.shape)
    shp[axis] = n
    return a.broadcast_to(shp)


def build(layers, n_ctx_blk, n_lat_blk, off, R, gather, final_norm=True, n_cores=8):
    NB = n_ctx_blk + n_lat_blk
    NT = NB * 128
    nc = bass.Bass("TRN2", target_bir_lowering=False)
    x_in = nc.dram_tensor("x_in", [NT, D], F32, kind="ExternalInput").ap()
    cvec = nc.dram_tensor("cvec", [2, D], F32, kind="ExternalInput").ap()
    y_out = nc.dram_tensor("y_out", [n_lat_blk * 128, D], F32, kind="ExternalOutput").ap()
    if gather:
        wsh = nc.dram_tensor("wsh", [R // n_cores, D], F32, kind="ExternalInput").ap()
        wsrc = nc.dram_tensor("wsrc", [R // n_cores, D], F32).ap()
        W = nc.dram_tensor("wall", [R, D], F32).ap()
    else:
        W = nc.dram_tensor("wall", [R, D], F32, kind="ExternalInput").ap()
    X = nc.dram_tensor("X", [NT, D], F32).ap()
    ub16 = nc.dram_tensor("ub16", [16384, D], BF16).ap()
    vb16 = nc.dram_tensor("vb16", [16384, D], BF16).ap()
    S_d = nc.dram_tensor("S_d", [NT, 2048], F32).ap()
    ST_d = nc.dram_tensor("ST_d", [NT, 16], F32).ap()
    HT_d = nc.dram_tensor("HT_d", [NB * 128, D], BF16).ap()
    has_ssm = any(i % 2 == 1 for i in layers)
    if has_ssm:
        UT_d = [nc.dram_tensor(f"UT_d{d}", [NB * 128, D], BF16).ap() for d in range(2)]
        U_d = nc.dram_tensor("U_d", [NT, D], F32).ap()
        Y_d = [nc.dram_tensor(f"Y_d{d}", [NT, D], F32).ap() for d in range(2)]

    with ExitStack() as top:
        P = Prog(nc, top, readonly=("wall", "x_in", "cvec"))
        _ctr = [0]

        def sb(name, shape, dt, st=top):
            _ctr[0] += 1
            return st.enter_context(nc.sbuf_tensor(f"{name}_{_ctr[0]}", shape, dt))
        ps = [top.enter_context(nc.psum_tensor(f"psum{j}", [128, 1024], F32)) for j in range(4)]

        ident = sb("ident", [128, 128], F32)
        jrev = sb("jrev", [128, 128], F32)
        identb = sb("identb", [128, 128], BF16)
        ones = sb("ones", [128, 128], F32)
        P.G(lambda e: e.memset(ones[:], 1.0))
        P.G(lambda e: e.affine_select(out=ident[:], in_=ones[:], pattern=[[1, 128]], compare_op=ALU.is_equal,
                                      fill=0.0, base=0, channel_multiplier=-1))
        P.G(lambda e: e.affine_select(out=jrev[:], in_=ones[:], pattern=[[1, 128]], compare_op=ALU.is_equal,
                                      fill=0.0, base=-127, channel_multiplier=1))
        P.V(lambda e: e.tensor_copy(out=identb[:], in_=ident[:]))

        if gather:
            P.dma(wsrc[:, :], wsh[:, :])
            P.emit("gpsimd", lambda e: e.collective_compute(
                "AllGather", ALU.bypass, ins=[wsrc[:, :]], outs=[W[:, :]],
                replica_groups=[list(range(n_cores))]), dma=True)

        def is_ctx(b):
            return b < n_ctx_blk

        xt = sb("xt", [128, D], F32)
        xs = sb("xs", [128, D], F32)
        tmp = sb("tmp", [128, D], F32)
        hT = sb("hT", [128, 8, 128], BF16)
        st1 = sb("st1", [128, 8], F32)
        scT = sb("scT", [128, 8, 2], F32)
        modT = sb("modT", [128, 48, 2], F32)
        gB = sb("gB", [128, 2, 2, D], F32)
        gwT = sb("gwT", [128, 2, 8, 2], F32)

        for v in range(2):
            P.dma(scT[:, :, v], cvec[v:v + 1, :].rearrange("o (k p) -> p (o k)", p=128), allow_slow_non_contiguous=True)
        P.act(scT[:], scT[:], AF.Silu)

        for b in range(NB):
            P.dma(xt[:], x_in[b * 128:(b + 1) * 128, :])
            if not is_ctx(b):
                lb = b - n_ctx_blk
                P.dma(xs[:], W[off["pos"] + lb * 128: off["pos"] + (lb + 1) * 128, :])
                P.V(lambda e: e.tensor_add(out=xt[:], in0=xt[:], in1=xs[:]))
            P.dma(X[b * 128:(b + 1) * 128, :], xt[:])

        def load_cast(dst_bf, src_rows, stg):
            P.dma(stg, src_rows.rearrange("(k p) c -> p k c", p=128))
            P.act(dst_bf, stg, AF.Copy)

        def modulation(i, stg, ls):
            scR = sb("scR", [128, 2, 8, 128], F32, ls)
            ngT = sb("ngT", [128, 2, 8], F32, ls)
            abT = sb("abT", [128, 48], F32, ls)
            abB = sb("abB", [128, 2, D], F32, ls)
            for v in range(2):
                P.V(lambda e, v=v: e.tensor_copy(out=scR[:, v], in_=bc(scT[:, :, v], 2, 128)))
            P.dma(abT[:], W[off[f"ada_b{i}"]:off[f"ada_b{i}"] + 6, :].rearrange("s (k p) -> p (s k)", p=128),
                  allow_slow_non_contiguous=True)
            for w, seg in enumerate((2, 5)):
                P.dma(abB[:, w, :], W[off[f"ada_b{i}"] + seg: off[f"ada_b{i}"] + seg + 1, :].partition_broadcast(128))
            for nn, nm in enumerate((f"n1g{i}", f"n2g{i}")):
                P.dma(ngT[:, nn, :], W[off[nm]:off[nm] + 1, :].rearrange("o (k p) -> p (o k)", p=128),
                      allow_slow_non_contiguous=True)
            for seg in range(6):
                r0 = off[f"ada_w{i}"] + seg * D
                P.dma(stg, W[r0:r0 + D, :].rearrange("(k p) c -> p k c", p=128))
                for kc in range(8):
                    for kd in range(8):
                        P.mm(ps[0][:, (seg * 8 + kc) * 2:(seg * 8 + kc) * 2 + 2], stg[:, kd, kc * 128:(kc + 1) * 128],
                             scT[:, kd, :], start=(kd == 0), stop=(kd == 7))
                if seg in (2, 5):
                    w = 0 if seg == 2 else 1
                    for v in range(2):
                        for n in range(2):
                            for kd in range(8):
                                P.mm(ps[1 + v][:, n * 512:(n + 1) * 512], scR[:, v, kd, :],
                                     stg[:, kd, n * 512:(n + 1) * 512], start=(kd == 0), stop=(kd == 7))
                        P.V(lambda e, v=v, w=w: e.tensor_add(out=gB[:, w, v, :], in0=ps[1 + v][:], in1=abB[:, w, :]))
            P.V(lambda e: e.tensor_add(out=modT[:], in0=ps[0][:, 0:96].rearrange("p (s v) -> p s v", v=2),
                                       in1=bc(abT[:], 2, 2)))
            for nn, sseg in enumerate((1, 4)):
                P.V(lambda e, nn=nn, sseg=sseg: e.scalar_tensor_tensor(
                    out=gwT[:, nn], in0=modT[:, sseg * 8:(sseg + 1) * 8, :], scalar=1.0,
                    in1=bc(ngT[:, nn, :], 2, 2), op0=ALU.add, op1=ALU.mult))

        def norm_T(nn, v, dst, src=None, rev=False):
            src = xt if src is None else src
            sseg = 0 if nn == 0 else 3
            P.act(xs[:], src[:], AF.Square, accum_out=st1[:, 0:1])
            P.act(st1[:, 1:2], st1[:, 0:1], AF.Sqrt, scale=1.0 / D, bias=epsb[:, 0:1])
            P.V(lambda e: e.reciprocal(out=st1[:, 2:3], in_=st1[:, 1:2]))
            P.act(xs[:], src[:], AF.Copy, scale=st1[:, 2:3])
            pt = ps[0][:].rearrange("p (k t) -> p k t", t=128)
            for k in range(8):
                P.mm(pt[:, k, :], xs[:, k * 128:(k + 1) * 128], (jrev if rev else ident)[:])
            for k in range(8):
                P.act(dst[:, k, :], pt[:, k, :], AF.Identity, scale=gwT[:, nn, k, v:v + 1],
                      bias=modT[:, sseg * 8 + k, v:v + 1])

        epsb = sb("epsb", [128, 1], F32)
        P.G(lambda e: e.memset(epsb[:], EPS))

        for li, i in enumerate(layers):
            last = (i == 3)
            with P.phase() as ls:
                stg = sb(f"stg{i}", [128, 8, D], F32, ls)
                modulation(i, stg[:], ls)
            C = SimpleNamespace(**{k: v for k, v in locals().items() if not k.startswith('_')})
            if i % 2 == 0:
                chunk_layer(C, i)
            else:
                ssm_layer(C, i, last)
            C = SimpleNamespace(**{k: v for k, v in locals().items() if not k.startswith('_')})
            peer_layer(C, i, last)

        fg = sb("fg", [128, D], F32)
        P.dma(fg[:], W[off["final_g"]:off["final_g"] + 1, :].partition_broadcast(128))
        for b in range(n_ctx_blk, NB):
            P.dma(xt[:], X[b * 128:(b + 1) * 128, :])
            if final_norm:
                P.act(xs[:], xt[:], AF.Square, accum_out=st1[:, 0:1])
                P.act(st1[:, 1:2], st1[:, 0:1], AF.Sqrt, scale=1.0 / D, bias=epsb[:, 0:1])
                P.V(lambda e: e.reciprocal(out=st1[:, 2:3], in_=st1[:, 1:2]))
                P.V(lambda e: e.scalar_tensor_tensor(out=xs[:], in0=xt[:], scalar=st1[:, 2:3], in1=fg[:],
                                                     op0=ALU.mult, op1=ALU.mult))
                P.dma(y_out[(b - n_ctx_blk) * 128:(b - n_ctx_blk + 1) * 128, :], xs[:])
            else:
                P.dma(y_out[(b - n_ctx_blk) * 128:(b - n_ctx_blk + 1) * 128, :], xt[:])
        P.finish()
    return nc


def chunk_layer(C, i):
    nc, P, sb, ps, off, W, X, NB, is_ctx = C.nc, C.P, C.sb, C.ps, C.off, C.W, C.X, C.NB, C.is_ctx
    xt, xs, tmp, hT, gB, norm_T, load_cast, epsb = C.xt, C.xs, C.tmp, C.hT, C.gB, C.norm_T, C.load_cast, C.epsb
    with P.phase() as cs:
        stg = sb("cstg", [128, 8, D], F32, cs)
        w_in = sb("cw_in", [128, 8, 2048], BF16, cs)
        w_out = sb("cw_out", [128, 8, D], BF16, cs)
        wsT = sb("cwsT", [128, 8, 128], BF16, cs)
        wsTf = sb("cwsTf", [128, 8, 128], F32, cs)
        lng = sb("clng", [128, D], F32, cs)
        lnb = sb("clnb", [128, D], F32, cs)
        bsb = sb("cbsb", [128, D], F32, cs)
        uT = sb("cuT", [128, D], F32, cs)
        vv = sb("cvv", [128, D], F32, cs)
        vln = sb("cvln", [128, D], BF16, cs)
        mT = sb("cmT", [128, 8, 128], BF16, cs)
        bst = sb("cbst", [128, 2, 6], F32, cs)
        mv = sb("cmv", [128, 4], F32, cs)
        o = off[f"cm_w_in{i}"]
        for m in range(2):
            load_cast(w_in[:, :, m * D:(m + 1) * D], W[o + m * D:o + (m + 1) * D, :], stg[:])
        o = off[f"cm_w_out{i}"]
        load_cast(w_out[:], W[o:o + D, :], stg[:])
        o = off[f"cm_wsT{i}"]
        P.dma(wsTf[:], W[o:o + 128, :].rearrange("r (a p) -> (r a) p", p=128).rearrange("(h q) p -> q h p", q=128))
        P.V(lambda e: e.tensor_copy(out=wsT[:], in_=wsTf[:]))
        for t_, nm in ((lng, "cm_ln_g"), (lnb, "cm_ln_b"), (bsb, "cm_bs")):
            o = off[f"{nm}{i}"]
            P.dma(t_[:], W[o:o + 1, :].partition_broadcast(128))
        xts = [xt, sb("cxt2", [128, D], F32, cs)]
        uTs = [uT, sb("cuT2", [128, D], F32, cs)]
        vvs = [vv, sb("cvv2", [128, D], F32, cs)]

        def front(b):
            v = 1 if is_ctx(b) else 0
            xb, uT_, vv_ = xts[b % 2], uTs[b % 2], vvs[b % 2]
            P.dma(xb[:], X[b * 128:(b + 1) * 128, :])
            norm_T(0, v, hT, src=xb)
            pu = ps[1][:].rearrange("p (f t) -> p f t", t=128)
            for f in range(8):
                for kd in range(8):
                    P.mm(pu[:, f, :], w_in[:, kd, f * 128:(f + 1) * 128], hT[:, kd, :], start=(kd == 0), stop=(kd == 7))
            for n in range(2):
                P.act(uT_[:, n * 512:(n + 1) * 512], ps[1][:, n * 512:(n + 1) * 512], AF.Gelu)
            for n in range(2):
                for kd in range(8):
                    P.mm(ps[2][:, n * 512:(n + 1) * 512], hT[:, kd, :], w_in[:, kd, D + n * 512:D + (n + 1) * 512],
                         start=(kd == 0), stop=(kd == 7))
            for n in range(2):
                P.act(vv_[:, n * 512:(n + 1) * 512], ps[2][:, n * 512:(n + 1) * 512], AF.Gelu)

        def ln_sp(b):
            vv_ = vvs[b % 2]
            for n in range(2):
                P.V(lambda e, n=n: e.bn_stats(out=bst[:, n, :], in_=vv_[:, n * 512:(n + 1) * 512]))
            P.V(lambda e: e.bn_aggr(out=mv[:, 0:2], in_=bst[:]))
            P.act(mv[:, 2:3], mv[:, 1:2], AF.Sqrt, scale=1.0, bias=epsb[:, 0:1])
            P.V(lambda e: e.reciprocal(out=mv[:, 3:4], in_=mv[:, 2:3]))
            P.V(lambda e: e.tensor_scalar(out=vv_[:], in0=vv_[:], scalar1=mv[:, 0:1], scalar2=mv[:, 3:4],
                                          op0=ALU.subtract, op1=ALU.mult))
            P.V(lambda e: e.tensor_mul(out=vv_[:], in0=vv_[:], in1=lng[:]))
            P.V(lambda e: e.tensor_add(out=vln[:], in0=vv_[:], in1=lnb[:]))
            pss = ps[3][:].rearrange("p (h t) -> p h t", t=128)
            for h in range(8):
                P.mm(pss[:, h, :], vln[:, h * 128:(h + 1) * 128], wsT[:, h, :])

        def back(b):
            v = 1 if is_ctx(b) else 0
            xb, uT_ = xts[b % 2], uTs[b % 2]
            P.V(lambda e: e.tensor_add(out=tmp[:], in0=ps[3][:], in1=bsb[:]))
            P.V(lambda e: e.tensor_mul(out=mT[:].rearrange("p h t -> p (h t)"), in0=tmp[:], in1=uT_[:]))
            for n in range(2):
                for h in range(8):
                    P.mm(ps[3][:, n * 512:(n + 1) * 512], mT[:, h, :], w_out[:, h, n * 512:(n + 1) * 512],
                         start=(h == 0), stop=(h == 7))
            P.V(lambda e: e.tensor_mul(out=tmp[:], in0=ps[3][:], in1=gB[:, 0, v, :]))
            P.V(lambda e: e.tensor_add(out=xb[:], in0=xb[:], in1=tmp[:]))
            P.dma(X[b * 128:(b + 1) * 128, :], xb[:])

        front(0)
        for b in range(NB):
            ln_sp(b)
            if b + 1 < NB:
                front(b + 1)
            back(b)


def peer_layer(C, i, last):
    nc, P, sb, ps, off, W, X, NB, is_ctx = C.nc, C.P, C.sb, C.ps, C.off, C.W, C.X, C.NB, C.is_ctx
    xt, xs, tmp, hT, gB, norm_T, load_cast = C.xt, C.xs, C.tmp, C.hT, C.gB, C.norm_T, C.load_cast
    ub16, vb16, S_d, ST_d, HT_d, identb = C.ub16, C.vb16, C.S_d, C.ST_d, C.HT_d, C.identb
    blocks = [b for b in range(NB) if not (last and is_ctx(b))]
    with P.phase() as c0:
        stgs = [sb("pstg", [128, 8, D], F32, c0) for _ in range(2)]
        stbs = [sb("pstb", [128, 8, D], BF16, c0) for _ in range(2)]
        k_ = 0
        for name, dst in ((f"uT{i}", ub16), (f"v{i}", vb16)):
            for sc in range(16):
                o = off[name] + sc * D
                P.dma(stgs[k_ % 2][:], W[o:o + D, :].rearrange("(k p) c -> p k c", p=128))
                P.act(stbs[k_ % 2][:], stgs[k_ % 2][:], AF.Copy)
                P.dma(dst[sc * D:(sc + 1) * D, :].rearrange("(k p) c -> p k c", p=128), stbs[k_ % 2][:])
                k_ += 1
    with P.phase() as cs:
        Weff = sb("pWeff", [128, 8, 2048], BF16, cs)
        with P.phase() as c1:
            stg = sb("pstg1", [128, 8, D], F32, c1)
            wqT = sb("pwqT", [128, 32, D], BF16, c1)
            keysT = sb("pkeysT", [128, 32, 128], BF16, c1)
            for m in range(4):
                o = off[f"wqT{i}"] + m * D
                load_cast(wqT[:, m * 8:(m + 1) * 8, :], W[o:o + D, :], stg[:])
            o = off[f"keysT{i}"]
            kf = stg[:].rearrange("p k c -> p (k c)")[:, 0:4096].rearrange("p (a m) -> p a m", m=128)
            P.dma(kf, W[o:o + 512, :].rearrange("r (a m) -> (r a) m", m=128).rearrange("(a e) m -> e a m", e=128))
            P.act(keysT[:], kf, AF.Copy)
            for dk in range(8):
                for hq in range(4):
                    bank = ps[1 + (hq % 2)][:, 0:512]
                    pw = bank.rearrange("p (a m) -> p a m", m=128)
                    for a_ in range(4):
                        hk = hq * 4 + a_
                        for j in range(2):
                            P.mm(pw[:, a_, :], wqT[:, hk * 2 + j, dk * 128:(dk + 1) * 128], keysT[:, hk * 2 + j, :],
                                 start=(j == 0), stop=(j == 1))
                    P.act(Weff[:, dk, hq * 512:(hq + 1) * 512], bank, AF.Copy)
        ss = [sb("ps_", [128, 16, 128], F32, cs) for _ in range(2)]
        hTs = [sb("phT1", [128, 8, 128], BF16, cs) for _ in range(2)]
        s2 = sb("ps2", [128, 128], F32, cs)
        T16 = sb("pT16", [128, 16, 16], F32, cs)
        cand = sb("pcand", [128, 8, 256], F32, cs)
        cand2 = sb("pcand2", [128, 256], F32, cs)
        TS = sb("pTS", [128, 8, 16], F32, cs)
        ex = sb("pex", [128, 8, 16], F32, cs)
        stat = sb("pstat", [128, 16], F32, cs)
        zz = sb("pzz", [128, 8], F32, cs)

        def front(bi):
            b = blocks[bi]
            hT_, s = hTs[bi % 2], ss[bi % 2]
            v = 1 if is_ctx(b) else 0
            P.dma(xt[:], X[b * 128:(b + 1) * 128, :])
            norm_T(1, v, hT_)
            P.dma(HT_d[b * 128:(b + 1) * 128, :].rearrange("p (k t) -> p k t", t=128), hT_[:])
            for half in range(2):
                for n in range(2):
                    col = half * 1024 + n * 512
                    for kd in range(8):
                        P.mm(ps[1 + half][:, n * 512:(n + 1) * 512], hT_[:, kd, :], Weff[:, kd, col:col + 512],
                             start=(kd == 0), stop=(kd == 7))
                    P.act(s[:, half * 8 + n * 4:half * 8 + n * 4 + 4, :],
                          ps[1 + half][:, n * 512:(n + 1) * 512].rearrange("p (a m) -> p a m", m=128), AF.Copy)
            P.dma(S_d[b * 128:(b + 1) * 128, :], s[:].rearrange("p a m -> p (a m)"))

        def back(bi):
            b = blocks[bi]
            s = ss[bi % 2]
            for hk in range(16):
                P.V(lambda e, hk=hk: e.max(out=T16[:, hk, 0:8], in_=s[:, hk, :]))
                P.V(lambda e, hk=hk: e.match_replace(out=s2[:], in_to_replace=T16[:, hk, 0:8], in_values=s[:, hk, :],
                                                     imm_value=NEG))
                P.V(lambda e, hk=hk: e.max(out=T16[:, hk, 8:16], in_=s2[:]))
            T4 = T16[:].rearrange("p (h k) a -> p h k a", k=2)
            P.V(lambda e: e.tensor_tensor(out=cand[:].rearrange("p h (a b) -> p h a b", b=16),
                                          in0=bc(T4[:, :, 0, :], 3, 16), in1=bc(T4[:, :, 1, :], 2, 16), op=ALU.add))
            for h in range(8):
                P.V(lambda e, h=h: e.max(out=TS[:, h, 0:8], in_=cand[:, h, :]))
                P.V(lambda e, h=h: e.match_replace(out=cand2[:], in_to_replace=TS[:, h, 0:8], in_values=cand[:, h, :],
                                                   imm_value=NEG))
                P.V(lambda e, h=h: e.max(out=TS[:, h, 8:16], in_=cand2[:]))
            P.V(lambda e: e.tensor_tensor(out=ex[:], in0=TS[:], in1=bc(TS[:, :, 0], 2, 16), op=ALU.subtract))
            P.act(ex[:], ex[:], AF.Exp)
            P.V(lambda e: e.tensor_reduce(out=zz[:], in_=ex[:], axis=AX.X, op=ALU.add))
            P.act(zz[:], zz[:], AF.Ln)
            P.V(lambda e: e.tensor_copy(out=stat[:, 0:8], in_=TS[:, :, 15]))
            P.V(lambda e: e.scalar_tensor_tensor(out=stat[:, 8:16], in0=TS[:, :, 0], scalar=-1.0, in1=zz[:],
                                                 op0=ALU.mult, op1=ALU.subtract))
            P.dma(ST_d[b * 128:(b + 1) * 128, :], stat[:])

        front(0)
        for bi in range(len(blocks)):
            if bi + 1 < len(blocks):
                front(bi + 1)
            back(bi)
    DELTA = 1e-5
    GBK = 2
    HP = 7
    with P.phase() as cs:
        UTc = [sb("pUTc", [128, 8, 512], BF16, cs) for _ in range(3)]
        Vc = [sb("pVc", [128, 4, D], BF16, cs) for _ in range(3)]
        sS = [sb("psS", [128, 16, 128], F32, cs) for _ in range(GBK)]
        statb = [sb("pstatb", [128, 16], F32, cs) for _ in range(GBK)]
        cst = [sb("pcst", [128, 8], F32, cs) for _ in range(GBK)]
        hTg = [sb("phTg", [128, 8, 128], BF16, cs) for _ in range(GBK)]
        rr_ = [sb("prr", [128, 8, 512], F32, cs) for _ in range(2)]
        wv = [sb("pwv", [128, 8, 512], BF16, cs) for _ in range(2)]
        mk = [sb("pmk", [128, 8, 512], BF16, cs) for _ in range(2)]
        t4 = sb("pt4", [128, 4, 512], BF16, cs)
        t2 = sb("pt2", [128, 2, 512], BF16, cs)
        Gs = [sb("pGs", [128, 512], BF16, cs) for _ in range(2)]
        ad = [sb("pad", [128, 512], F32, cs) for _ in range(2)]
        A = [sb("pA", [128, 512], BF16, cs) for _ in range(2)]
        AT = [sb("pAT", [128, 4, 128], BF16, cs) for _ in range(2)]
        psd = [ps[0][:, 0:512], ps[0][:, 512:1024]]
        psT = [ps[1][:, 0:512], ps[1][:, 512:1024]]
        pso = [ps[2], ps[3]]
        xg = [xt, xs]
        groups = [blocks[k:k + GBK] for k in range(0, len(blocks), GBK)]
        for grp in groups:
            ng = len(grp)
            for gi, b in enumerate(grp):
                P.dma(hTg[gi][:], HT_d[b * 128:(b + 1) * 128, :].rearrange("p (k t) -> p k t", t=128))
                P.dma(sS[gi][:].rearrange("p a m -> p (a m)"), S_d[b * 128:(b + 1) * 128, :])
                P.dma(statb[gi][:], ST_d[b * 128:(b + 1) * 128, :])
                s4 = sS[gi][:].rearrange("p (h k) m -> p h k m", k=2)
                P.V(lambda e, gi=gi, s4=s4: e.tensor_tensor(out=s4[:, :, 0, :], in0=s4[:, :, 0, :],
                                                            in1=bc(statb[gi][:, 0:8], 2, 128), op=ALU.subtract))
                P.V(lambda e, s4=s4: e.tensor_scalar_add(out=s4[:, :, 0, :], in0=s4[:, :, 0, :], scalar1=DELTA))
                P.V(lambda e, gi=gi: e.tensor_tensor(out=cst[gi][:], in0=statb[gi][:, 0:8], in1=statb[gi][:, 8:16],
                                                     op=ALU.add))
                P.V(lambda e, gi=gi: e.tensor_scalar_add(out=cst[gi][:], in0=cst[gi][:], scalar1=-DELTA))
            chunks = [(nch, gi) for nch in range(32) for gi in range(ng)]
            N = len(chunks)

            def T(nch):
                sc, half = nch // 2, nch % 2
                P.dma(UTc[nch % 3][:], ub16[sc * D:(sc + 1) * D, half * 512:(half + 1) * 512].rearrange(
                    "(k p) c -> p k c", p=128))
                r0 = nch * 512
                P.dma(Vc[nch % 3][:], vb16[r0:r0 + 512, :].rearrange("(c j) d -> j c d", j=128))

            def R(c):
                nch, gi = chunks[c]
                p = c % 2
                i0 = nch * 4
                s4 = sS[gi][:].rearrange("p (h k) m -> p h k m", k=2)
                rv = rr_[p][:].rearrange("p h (i j) -> p h i j", j=128)
                P.V(lambda e: e.tensor_tensor(out=rv, in0=bc(s4[:, :, 0, i0:i0 + 4], 3, 128),
                                              in1=bc(s4[:, :, 1, :], 2, 4), op=ALU.add))

            def Wst(c):
                nch, gi = chunks[c]
                p = c % 2
                for h in range(8):
                    P.act(wv[p][:, h, :], rr_[p][:, h, :], AF.Exp, bias=cst[gi][:, h:h + 1], scale=1.0)

            def M(c):
                p = c % 2
                P.V(lambda e: e.scalar_tensor_tensor(out=mk[p][:].rearrange("p h x -> p (h x)"),
                                                     in0=rr_[p][:].rearrange("p h x -> p (h x)"), scalar=0.0,
                                                     in1=wv[p][:].rearrange("p h x -> p (h x)"),
                                                     op0=ALU.is_ge, op1=ALU.mult))
                P.V(lambda e: e.tensor_add(out=t4[:], in0=mk[p][:, 0:4, :], in1=mk[p][:, 4:8, :]))
                P.V(lambda e: e.tensor_add(out=t2[:], in0=t4[:, 0:2, :], in1=t4[:, 2:4, :]))
                P.V(lambda e: e.tensor_add(out=Gs[p][:], in0=t2[:, 0, :], in1=t2[:, 1, :]))

            def S1(c):
                nch, gi = chunks[c]
                p = c % 2
                for kd in range(8):
                    P.mm(psd[p], hTg[gi][:, kd, :], UTc[nch % 3][:, kd, :], start=(kd == 0), stop=(kd == 7))

            def S2(c):
                p = c % 2
                P.act(ad[p][:], psd[p], AF.Gelu)

            def S3(c):
                p = c % 2
                P.V(lambda e: e.tensor_mul(out=A[p][:], in0=ad[p][:], in1=Gs[p][:]))

            def S456(c):
                nch, gi = chunks[c]
                p = c % 2
                pT = psT[p].rearrange("p (c t) -> p c t", t=128)
                for cc in range(4):
                    P.mm(pT[:, cc, :], A[p][:, cc * 128:(cc + 1) * 128], identb[:])
                P.act(AT[p][:], pT, AF.Copy)
                for cc in range(4):
                    for nn in range(2):
                        P.mm(pso[gi][:, nn * 512:(nn + 1) * 512], AT[p][:, cc, :],
                             Vc[nch % 3][:, cc, nn * 512:(nn + 1) * 512],
                             start=(nch == 0 and cc == 0), stop=(nch == 31 and cc == 3))

            T(0)
            T(1)
            R(0)
            Wst(0)
            S1(0)
            S2(0)
            for c in range(N):
                nch, gi = chunks[c]
                if gi == 0 and nch + 2 < 32:
                    T(nch + 2)
                if c + 1 < N:
                    R(c + 1)
                    Wst(c + 1)
                    S1(c + 1)
                    S2(c + 1)
                M(c)
                S3(c)
                S456(c)
            for gi, b in enumerate(grp):
                v = 1 if is_ctx(b) else 0
                P.dma(xg[gi][:], X[b * 128:(b + 1) * 128, :])
                P.V(lambda e, gi=gi, v=v: e.tensor_mul(out=tmp[:], in0=pso[gi][:], in1=gB[:, 1, v, :]))
                P.V(lambda e, gi=gi: e.tensor_add(out=xg[gi][:], in0=xg[gi][:], in1=tmp[:]))
                P.dma(X[b * 128:(b + 1) * 128, :], xg[gi][:])


def ssm_layer(C, i, last):
    nc, P, sb, ps, off, W, X, NB, is_ctx = C.nc, C.P, C.sb, C.ps, C.off, C.W, C.X, C.NB, C.is_ctx
    xt, xs, tmp, hT, gB, norm_T, load_cast = C.xt, C.xs, C.tmp, C.hT, C.gB, C.norm_T, C.load_cast
    UT_d, U_d, Y_d, ident, jrev, n_ctx_blk = C.UT_d, C.U_d, C.Y_d, C.ident, C.jrev, C.n_ctx_blk

    def mirror(b):
        return (n_ctx_blk - 1 - b) if is_ctx(b) else (n_ctx_blk + (NB - 1 - b))

    with P.phase() as cs:
        stg = sb("sstg", [128, 8, D], F32, cs)
        w_in = sb("sw_in", [128, 8, D], BF16, cs)
        hTr = sb("shTr", [128, 8, 128], BF16, cs)
        uTb = sb("suTb", [128, 8, 128], BF16, cs)
        o = off[f"ssm_w_in{i}"]
        load_cast(w_in[:], W[o:o + D, :], stg[:])
        for b in range(NB):
            v = 1 if is_ctx(b) else 0
            P.dma(xt[:], X[b * 128:(b + 1) * 128, :])
            norm_T(0, v, hT)
            norm_T(0, v, hTr, rev=True)
            for n in range(2):
                for kd in range(8):
                    P.mm(ps[1][:, n * 512:(n + 1) * 512], hT[:, kd, :], w_in[:, kd, n * 512:(n + 1) * 512],
                         start=(kd == 0), stop=(kd == 7))
            P.V(lambda e: e.tensor_copy(out=tmp[:], in_=ps[1][:]))
            P.dma(U_d[b * 128:(b + 1) * 128, :], tmp[:])
            for d, hsrc, pos in ((0, hT, b), (1, hTr, mirror(b))):
                pu = ps[2][:].rearrange("p (f t) -> p f t", t=128)
                for f in range(8):
                    for kd in range(8):
                        P.mm(pu[:, f, :], w_in[:, kd, f * 128:(f + 1) * 128], hsrc[:, kd, :], start=(kd == 0), stop=(kd == 7))
                P.V(lambda e, pu=pu: e.tensor_copy(out=uTb[:], in_=pu))
                P.dma(UT_d[d][pos * 128:(pos + 1) * 128, :].rearrange("p (k t) -> p k t", t=128), uTb[:])

    with P.phase() as cs:
        Wr = sb("sWr", [128, 32, 128], F32, cs)
        Wi = sb("sWi", [128, 32, 128], F32, cs)
        Rr = sb("sRr", [128, 32, 128], F32, cs)
        Ri = sb("sRi", [128, 32, 128], F32, cs)
        Rho0 = sb("sRho0", [128, 32, 128], F32, cs)
        BTr = sb("sBTr", [128, 8, 4, 128], BF16, cs)
        BTi = sb("sBTi", [128, 8, 4, 128], BF16, cs)
        CTr = sb("sCTr", [128, 32, 32], BF16, cs)
        CTi = sb("sCTi", [128, 32, 32], BF16, cs)
        big = sb("sbig", [128, 6, 1024], F32, cs)
        wk = [big[:, j, :] for j in range(6)]
        hb = [sb(f"shb{j}", [128, 1024], BF16, cs) for j in range(4)]
        CTrn = sb("sCTrn", [128, 32, 32], BF16, cs)
        uTs = [sb("suT", [128, 8, 128], BF16, cs) for _ in range(2)]
        sm = [sb(f"ssm{j}", [128, 32], F32, cs) for j in range(12)]
        carS = [sb(f"scar{j}", [128, 32], F32, cs) for j in range(2)]
        ti32 = sb("sti32", [128, 32], mybir.dt.int32, cs)
        lr, li_, dl, rho, th, cs_, sn_, cr, ci, t0, t1_, t2_ = sm
        for d in range(2):
            with P.phase() as ss:
                stg = big[:].rearrange("p a c -> p (a c)")[:, 0:4096]
                for name, dst in ((f"bT_r{i}", BTr), (f"bT_i{i}", BTi)):
                    o = off[name] + d * 512
                    P.dma(stg[:], W[o:o + 512, :].rearrange("(p r) c -> p (r c)", p=128))
                    P.act(dst[:].rearrange("p a b c -> p (a b c)"), stg[:], AF.Copy)
                for name, dst, scl in ((f"cT_r{i}", CTr, 1.0), (f"cT_r{i}", CTrn, -1.0), (f"cT_i{i}", CTi, -1.0)):
                    o = off[name] + d * 128
                    P.dma(stg[:, 0:1024], W[o:o + 128, :])
                    P.act(dst[:].rearrange("p a b -> p (a b)"), stg[:, 0:1024], AF.Copy, scale=scl)
                for name, dst in ((f"lamre{i}", lr), (f"lamim{i}", li_), (f"lstep{i}", dl)):
                    o = off[name] + d * 4
                    P.dma(dst[:], W[o:o + 4, :].rearrange("r (a g) -> (r a) g", g=32))
                P.act(dl[:], dl[:], AF.Exp)
                P.V(lambda e: e.tensor_mul(out=rho[:], in0=lr[:], in1=dl[:]))
                P.act(rho[:], rho[:], AF.Exp)
                P.V(lambda e: e.tensor_mul(out=th[:], in0=li_[:], in1=dl[:]))
                for dst, shift in ((sn_, 0.0), (cs_, 0.5 * PI)):
                    P.V(lambda e, shift=shift: e.tensor_scalar_add(out=t1_[:], in0=th[:], scalar1=shift))
                    P.V(lambda e: e.tensor_scalar_mul(out=ti32[:], in0=t1_[:], scalar1=1.0 / (2.0 * PI)))
                    P.V(lambda e: e.tensor_copy(out=t0[:], in_=ti32[:]))
                    P.V(lambda e: e.scalar_tensor_tensor(out=t1_[:], in0=t0[:], scalar=-2.0 * PI, in1=t1_[:],
                                                         op0=ALU.mult, op1=ALU.add))
                    P.V(lambda e: e.tensor_scalar(out=t0[:], in0=t1_[:], scalar1=PI, scalar2=-2.0 * PI,
                                                  op0=ALU.is_gt, op1=ALU.mult))
                    P.V(lambda e: e.tensor_add(out=t1_[:], in0=t1_[:], in1=t0[:]))
                    P.V(lambda e: e.tensor_scalar(out=t0[:], in0=t1_[:], scalar1=-PI, scalar2=2.0 * PI,
                                                  op0=ALU.is_lt, op1=ALU.mult))
                    P.V(lambda e: e.tensor_add(out=t1_[:], in0=t1_[:], in1=t0[:]))
                    P.act(dst[:], t1_[:], AF.Sin)
                P.V(lambda e: e.tensor_mul(out=t0[:], in0=rho[:], in1=cs_[:]))
                P.V(lambda e: e.tensor_scalar_add(out=t0[:], in0=t0[:], scalar1=-1.0))
                P.V(lambda e: e.tensor_mul(out=t1_[:], in0=rho[:], in1=sn_[:]))
                P.V(lambda e: e.tensor_mul(out=cr[:], in0=t0[:], in1=lr[:]))
                P.V(lambda e: e.tensor_mul(out=t2_[:], in0=t1_[:], in1=li_[:]))
                P.V(lambda e: e.tensor_add(out=cr[:], in0=cr[:], in1=t2_[:]))
                P.V(lambda e: e.tensor_mul(out=ci[:], in0=t1_[:], in1=lr[:]))
                P.V(lambda e: e.tensor_mul(out=t2_[:], in0=t0[:], in1=li_[:]))
                P.V(lambda e: e.tensor_sub(out=ci[:], in0=ci[:], in1=t2_[:]))
                P.V(lambda e: e.tensor_mul(out=t0[:], in0=lr[:], in1=lr[:]))
                P.V(lambda e: e.tensor_mul(out=t1_[:], in0=li_[:], in1=li_[:]))
                P.V(lambda e: e.tensor_add(out=t0[:], in0=t0[:], in1=t1_[:]))
                P.V(lambda e: e.reciprocal(out=t0[:], in_=t0[:]))
                P.V(lambda e: e.tensor_mul(out=cr[:], in0=cr[:], in1=t0[:]))
                P.V(lambda e: e.tensor_mul(out=ci[:], in0=ci[:], in1=t0[:]))
                A1 = stg[:, 0:2048].rearrange("p (g k) -> p g k", k=64)
                A2 = stg[:, 2048:4096].rearrange("p (g k) -> p g k", k=64)
                P.V(lambda e: e.tensor_copy(out=Wr[:, :, 0], in_=cs_[:]))
                P.V(lambda e: e.tensor_copy(out=Wi[:, :, 0], in_=sn_[:]))
                s_ = 1
                while s_ < 128:
                    mr, mi = bc(Wr[:, :, s_ - 1], 2, s_), bc(Wi[:, :, s_ - 1], 2, s_)
                    a1, a2 = A1[:, :, 0:s_], A2[:, :, 0:s_]
                    P.V(lambda e, a1=a1, mr=mr, s_=s_: e.tensor_mul(out=a1, in0=Wr[:, :, 0:s_], in1=mr))
                    P.V(lambda e, a2=a2, mi=mi, s_=s_: e.tensor_mul(out=a2, in0=Wi[:, :, 0:s_], in1=mi))
                    P.V(lambda e, a1=a1, a2=a2, s_=s_: e.tensor_sub(out=Wr[:, :, s_:2 * s_], in0=a1, in1=a2))
                    P.V(lambda e, a1=a1, mi=mi, s_=s_: e.tensor_mul(out=a1, in0=Wr[:, :, 0:s_], in1=mi))
                    P.V(lambda e, a2=a2, mr=mr, s_=s_: e.tensor_mul(out=a2, in0=Wi[:, :, 0:s_], in1=mr))
                    P.V(lambda e, a1=a1, a2=a2, s_=s_: e.tensor_add(out=Wi[:, :, s_:2 * s_], in0=a1, in1=a2))
                    s_ *= 2
                B1 = stg[:].rearrange("p (g k) -> p g k", k=128)
                P.V(lambda e: e.tensor_mul(out=Rr[:], in0=Wr[:], in1=bc(cr[:], 2, 128)))
                P.V(lambda e: e.tensor_mul(out=B1, in0=Wi[:], in1=bc(ci[:], 2, 128)))
                P.V(lambda e: e.tensor_add(out=Rr[:], in0=Rr[:], in1=B1))
                P.V(lambda e: e.tensor_mul(out=Ri[:], in0=Wr[:], in1=bc(ci[:], 2, 128)))
                P.V(lambda e: e.tensor_mul(out=B1, in0=Wi[:], in1=bc(cr[:], 2, 128)))
                P.V(lambda e: e.tensor_sub(out=Ri[:], in0=Ri[:], in1=B1))
                P.V(lambda e: e.tensor_copy(out=Rho0[:], in_=bc(rho[:], 2, 128)))
                P.V(lambda e: e.memset(Rho0[:, :, 0], 0.0))
                P.V(lambda e: e.memset(carS[0][:], 0.0))
                P.V(lambda e: e.memset(carS[1][:], 0.0))
            steps = [(pos, q) for pos in range(NB) for q in range(4)]

            def load_u(pos):
                P.dma(uTs[pos % 2][:], UT_d[d][pos * 128:(pos + 1) * 128, :].rearrange("p (k t) -> p k t", t=128))

            def inject(pos, q):
                uT = uTs[pos % 2]
                Xr, Xi = ps[0][:], ps[1][:]
                for g8 in range(8):
                    gp = q * 8 + g8
                    ck, g4 = gp // 4, gp % 4
                    P.mm(Xr[:, g8 * 128:(g8 + 1) * 128], BTr[:, ck, g4, :], uT[:, ck, :])
                    P.mm(Xi[:, g8 * 128:(g8 + 1) * 128], BTi[:, ck, g4, :], uT[:, ck, :])

            def rotate_in(pos, q):
                Xr, Xi = ps[0][:], ps[1][:]
                gs = slice(q * 8, (q + 1) * 8)
                Rqr = Rr[:, gs, :].rearrange("p g k -> p (g k)")
                Rqi = Ri[:, gs, :].rearrange("p g k -> p (g k)")
                Wqr = Wr[:, gs, :].rearrange("p g k -> p (g k)")
                Wqi = Wi[:, gs, :].rearrange("p g k -> p (g k)")
                Rhq = Rho0[:, gs, :].rearrange("p g k -> p (g k)")
                k1, k2, btr, bti, gr, gi = wk
                hr, hi = btr, bti
                P.V(lambda e: e.tensor_mul(out=k1, in0=Xr, in1=Rqr))
                P.V(lambda e: e.tensor_mul(out=k2, in0=Xi, in1=Rqi))
                P.V(lambda e: e.tensor_sub(out=btr, in0=k1, in1=k2))
                P.V(lambda e: e.tensor_mul(out=k1, in0=Xi, in1=Rqr))
                P.V(lambda e: e.tensor_mul(out=k2, in0=Xr, in1=Rqi))
                P.V(lambda e: e.tensor_add(out=bti, in0=k1, in1=k2))

            def rest(pos, q):
                gs = slice(q * 8, (q + 1) * 8)
                Wqr = Wr[:, gs, :].rearrange("p g k -> p (g k)")
                Wqi = Wi[:, gs, :].rearrange("p g k -> p (g k)")
                Rhq = Rho0[:, gs, :].rearrange("p g k -> p (g k)")
                k1, k2, btr, bti, gr, gi = wk
                hr, hi = btr, bti
                b3r = btr.rearrange("p (g k) -> p g k", k=128)
                b3i = bti.rearrange("p (g k) -> p g k", k=128)
                P.V(lambda e: e.tensor_add(out=b3r[:, :, 0], in0=b3r[:, :, 0], in1=carS[0][:, gs]))
                P.V(lambda e: e.tensor_add(out=b3i[:, :, 0], in0=b3i[:, :, 0], in1=carS[1][:, gs]))
                P.V(lambda e: e.tensor_tensor_scan(out=gr, data0=Rhq, data1=btr, initial=0.0,
                                                   op0=ALU.mult, op1=ALU.add))
                P.V(lambda e: e.tensor_tensor_scan(out=gi, data0=Rhq, data1=bti, initial=0.0,
                                                   op0=ALU.mult, op1=ALU.add))
                P.V(lambda e: e.tensor_mul(out=k1, in0=gr, in1=Wqr))
                P.V(lambda e: e.tensor_mul(out=k2, in0=gi, in1=Wqi))
                P.V(lambda e: e.tensor_mul(out=btr, in0=gi, in1=Wqr))
                P.V(lambda e: e.tensor_mul(out=bti, in0=gr, in1=Wqi))
                e1, e2, e3, e4 = (t.rearrange("p (g k) -> p g k", k=128)[:, :, 127] for t in (k1, k2, btr, bti))
                P.V(lambda e: e.tensor_sub(out=t0[:, 0:8], in0=e1, in1=e2))
                P.V(lambda e: e.tensor_mul(out=carS[0][:, gs], in0=t0[:, 0:8], in1=rho[:, gs]))
                P.V(lambda e: e.tensor_add(out=t1_[:, 0:8], in0=e3, in1=e4))
                P.V(lambda e: e.tensor_mul(out=carS[1][:, gs], in0=t1_[:, 0:8], in1=rho[:, gs]))
                for j_, t in enumerate((k1, k2, btr, bti)):
                    P.act(hb[j_][:], t, AF.Copy)

            def readout(pos, q):
                for g8 in range(8):
                    gp = q * 8 + g8
                    for j_, ct in enumerate((CTr, CTrn, CTi, CTi)):
                        P.mm(ps[2][:, gp * 32:(gp + 1) * 32], hb[j_][:, g8 * 128:(g8 + 1) * 128], ct[:, gp, :],
                             start=(j_ == 0), stop=(j_ == 3))

            load_u(0)
            inject(*steps[0])
            for k, (pos, q) in enumerate(steps):
                if q == 0 and pos + 1 < NB:
                    load_u(pos + 1)
                rotate_in(pos, q)
                if k + 1 < len(steps):
                    inject(*steps[k + 1])
                rest(pos, q)
                readout(pos, q)
                if q == 3:
                    P.V(lambda e: e.tensor_copy(out=tmp[:], in_=ps[2][:]))
                    P.dma(Y_d[d][pos * 128:(pos + 1) * 128, :], tmp[:])

    with P.phase() as cs:
        stg = sb("sstg3", [128, 8, D], F32, cs)
        w_out = sb("sw_out", [128, 8, 2048], BF16, cs)
        dB = sb("sdB", [128, D], F32, cs)
        a1 = sb("sa1", [128, D], F32, cs)
        a2 = sb("sa2", [128, D], F32, cs)
        gyT = sb("sgyT", [128, 8, 128], BF16, cs)
        o = off[f"ssm_w_out{i}"]
        for m in range(2):
            load_cast(w_out[:, :, m * D:(m + 1) * D], W[o + m * D:o + (m + 1) * D, :], stg[:])
        o = off[f"ssm_d{i}"]
        P.dma(dB[:], W[o:o + 1, :].partition_broadcast(128))
        for b in range(NB):
            if last and is_ctx(b):
                continue
            v = 1 if is_ctx(b) else 0
            P.dma(xt[:], X[b * 128:(b + 1) * 128, :])
            P.dma(a1[:], U_d[b * 128:(b + 1) * 128, :])
            P.dma(tmp[:], Y_d[0][b * 128:(b + 1) * 128, :])
            P.dma(a2[:], Y_d[1][mirror(b) * 128:(mirror(b) + 1) * 128, :])
            P.V(lambda e: e.tensor_mul(out=a1[:], in0=a1[:], in1=dB[:]))
            P.V(lambda e: e.tensor_add(out=a1[:], in0=a1[:], in1=tmp[:]))
            pt = ps[0][:].rearrange("p (k t) -> p k t", t=128)
            for k in range(8):
                P.mm(pt[:, k, :], a1[:, k * 128:(k + 1) * 128], ident[:], start=True, stop=False)
                P.mm(pt[:, k, :], a2[:, k * 128:(k + 1) * 128], jrev[:], start=False, stop=True)
            for n in range(2):
                P.act(gyT[:].rearrange("p k t -> p (k t)")[:, n * 512:(n + 1) * 512], ps[0][:, n * 512:(n + 1) * 512], AF.Gelu)
            for n in range(4):
                po = ps[1 + n // 2][:, (n % 2) * 512:(n % 2 + 1) * 512]
                for k in range(8):
                    P.mm(po, gyT[:, k, :], w_out[:, k, n * 512:(n + 1) * 512], start=(k == 0), stop=(k == 7))
            for n in range(2):
                P.act(tmp[:, n * 512:(n + 1) * 512], ps[2][:, n * 512:(n + 1) * 512], AF.Sigmoid)
            P.V(lambda e: e.tensor_mul(out=tmp[:], in0=ps[1][:], in1=tmp[:]))
            P.V(lambda e, v=v: e.tensor_mul(out=tmp[:], in0=tmp[:], in1=gB[:, 0, v, :]))
            P.V(lambda e: e.tensor_add(out=xt[:], in0=xt[:], in1=tmp[:]))
            P.dma(X[b * 128:(b + 1) * 128, :], xt[:])


def run(inp, layers, n_cores, n_ctx, n_lat, final_norm=True, gather=None):
    gather = (n_cores > 1) if gather is None else gather
    Wall, off, R = pack_weights(inp, layers, n_lat)
    nc = build(layers, n_ctx // 128, n_lat // 128, off, R, gather, final_norm, n_cores)
    in_maps = []
    for b in range(n_cores):
        m = {"x_in": np.ascontiguousarray(np.concatenate([inp["ctx"][b, :n_ctx], inp["x"][b, :n_lat]], axis=0), dtype=np.float32),
             "cvec": np.ascontiguousarray(np.stack([inp["c"][b], inp["c_ctx"]]), dtype=np.float32)}
        if gather:
            m["wsh"] = np.ascontiguousarray(Wall[b * (R // n_cores):(b + 1) * (R // n_cores)])
        else:
            m["wall"] = Wall
        in_maps.append(m)
    import os
    if os.environ.get("KTRACE"):
        res = run_bass_kernel_spmd(nc, in_maps, core_ids=list(range(n_cores)), trace=True)
        print("EXEC_TIME_NS", res.exec_time_ns, flush=True)
    else:
        res = run_bass_kernel_spmd(nc, in_maps, core_ids=list(range(n_cores)))
    return np.stack([r["y_out"] for r in res.results], axis=0)


def kernel(**inputs):
    inp = {k: np.asarray(v) for k, v in inputs.items()}
    return run(inp, [0, 1, 2, 3], 8, 256, 8192, gather=False).astype(np.float32)
```

```python
import numpy as np
from contextlib import ExitStack
from types import SimpleNamespace
import concourse.bass as bass
import concourse.mybir as mybir
from concourse.bass_utils import run_bass_kernel_spmd

F32 = mybir.dt.float32
BF16 = mybir.dt.bfloat16
AF = mybir.ActivationFunctionType
ALU = mybir.AluOpType
AX = mybir.AxisListType
D = 1024
NEG = -1.0e30
EPS = 1e-6
PI = float(np.pi)


def _rows(a):
    a = np.ascontiguousarray(a, dtype=np.float32).reshape(-1)
    pad = (-a.size) % D
    if pad:
        a = np.concatenate([a, np.zeros(pad, np.float32)])
    return a.reshape(-1, D)


def _pos_embed(rows):
    def sincos(pos, dim):
        omega = (1.0 / (np.float32(10000.0) ** (np.arange(dim // 2, dtype=np.float32) / np.float32(dim // 2)))).astype(np.float32)
        ang = (pos[:, None] * omega[None, :]).astype(np.float32)
        return np.concatenate([np.sin(ang), np.cos(ang)], axis=-1).astype(np.float32)
    r = np.repeat(np.arange(rows, dtype=np.float32), 64)
    col = np.tile(np.arange(64, dtype=np.float32), rows)
    return np.concatenate([sincos(r, D // 2), sincos(col, D // 2)], axis=-1).astype(np.float32)


def pack_weights(inp, layers, n_lat):
    parts, off = [], {}
    cur = 0

    def add(name, arr):
        nonlocal cur
        r = _rows(arr)
        off[name] = cur
        parts.append(r)
        cur += r.shape[0]

    add("pos", _pos_embed(n_lat // 64))
    add("final_g", inp["final_g"])
    for i in layers:
        k = i // 2
        add(f"ada_w{i}", inp["ada_w"][i].reshape(D, 6, D).transpose(1, 0, 2))
        add(f"ada_b{i}", inp["ada_b"][i])
        add(f"n1g{i}", inp["norm1_g"][i])
        add(f"n2g{i}", inp["norm2_g"][i])
        add(f"wqT{i}", inp["peer_w_q"][i].T)
        add(f"keysT{i}", inp["peer_keys"][i].transpose(0, 1, 3, 2))
        add(f"uT{i}", inp["peer_u"][i].T.reshape(D, 16, D).transpose(1, 0, 2))
        add(f"v{i}", inp["peer_v"][i])
        if i % 2 == 0:
            add(f"cm_w_in{i}", inp["cm_w_in"][k].reshape(D, 2, D).transpose(1, 0, 2))
            add(f"cm_w_out{i}", inp["cm_w_out"][k])
            add(f"cm_ln_g{i}", inp["cm_ln_g"][k])
            add(f"cm_ln_b{i}", inp["cm_ln_b"][k])
            add(f"cm_wsT{i}", inp["cm_w_s"][k].transpose(0, 2, 1))
            add(f"cm_bs{i}", inp["cm_b_s"][k])
        else:
            add(f"ssm_w_in{i}", inp["ssm_w_in"][k])
            add(f"ssm_w_out{i}", inp["ssm_w_out"][k].reshape(D, 2, D).transpose(1, 0, 2))
            def sp(a):
                return a.reshape(2, 32, 128).transpose(0, 2, 1)
            add(f"lamre{i}", sp(inp["ssm_lam_re"][k]))
            add(f"lamim{i}", sp(inp["ssm_lam_im"][k]))
            add(f"lstep{i}", sp(np.broadcast_to(inp["ssm_log_step"][k][:, :, None], (2, 64, 64))))
            def bT(b):
                o = np.zeros((2, 128, 8, 4, 128), np.float32)
                for g in range(64):
                    ch0 = g * 16
                    chunk, cl = ch0 // 128, ch0 % 128
                    o[:, cl:cl + 16, chunk, (g // 2) % 4, (g % 2) * 64:(g % 2) * 64 + 64] = b[:, g].transpose(0, 2, 1)
                return o
            add(f"bT_r{i}", bT(inp["ssm_b_re"][k]))
            add(f"bT_i{i}", bT(inp["ssm_b_im"][k]))
            def cT(c):
                o = np.zeros((2, 128, 32, 32), np.float32)
                for g in range(64):
                    o[:, (g % 2) * 64:(g % 2) * 64 + 64, g // 2, (g % 2) * 16:(g % 2) * 16 + 16] = c[:, g].transpose(0, 2, 1)
                return o
            add(f"cT_r{i}", cT(inp["ssm_c_re"][k]))
            add(f"cT_i{i}", cT(inp["ssm_c_im"][k]))
            add(f"ssm_d{i}", inp["ssm_d"][k])
    pad = (-cur) % 8
    if pad:
        parts.append(np.zeros((pad, D), np.float32))
        cur += pad
    return np.concatenate(parts, axis=0), off, cur


def _is_ap(x):
    return hasattr(x, "ap") and hasattr(x, "offset") and hasattr(x, "space") and hasattr(x, "tensor")


def _region(a):
    steps = list(a.ap)
    sp = str(a.space)
    if sp in ("SB", "PSUM"):
        shape = list(a.tensor.shape)
        row = 1
        for d in shape[1:]:
            row *= d
        if steps and (steps[0][0] == row or steps[0][1] == 1):
            base = a.offset % row
            dims = steps[1:]
        else:
            return (a.name, 0, row - 1)
    else:
        base = a.offset
        dims = steps
    lo = base + sum(min(0, st * (c - 1)) for st, c in dims)
    hi = base + sum(max(0, st * (c - 1)) for st, c in dims)
    if sp == "PSUM":
        lo = (lo // 512) * 512
        hi = (hi // 512) * 512 + 511
    return (a.name, lo, hi)


class _Proxy:
    WRITE_KEYS = ("out", "accum_out", "out_max", "out_indices")

    def __init__(self, prog, eng):
        self.prog, self.eng = prog, eng

    def __getattr__(self, name):
        prog, eng = self.prog, self.eng
        real = getattr(getattr(prog.nc, eng), name)

        def call(*args, **kw):
            reads, writes = [], []
            for j, x in enumerate(args):
                if _is_ap(x):
                    (writes if j == 0 else reads).append(x)
            for k, x in kw.items():
                if _is_ap(x):
                    (writes if k in self.WRITE_KEYS else reads).append(x)
            prog._sync(eng, reads, writes)
            return real(*args, **kw)
        return call


class Prog:
    ENG = ("sync", "scalar", "vector", "gpsimd", "tensor")
    NDMA = 8

    def __init__(self, nc, stack, readonly=()):
        self.nc, self.stack = nc, stack
        self.sem, self.cnt, self.nsem = {}, {}, 0
        self.waited = {e: {} for e in self.ENG}
        self.Wr, self.Rd = {}, {}
        self.readonly = set(readonly)
        self.dpool, self.dcnt, self.dk = {}, {}, {}
        self.pending = None
        self.last = None

    def _newsem(self):
        s = self.stack.enter_context(self.nc.semaphore(f"s{self.nsem}"))
        self.nsem += 1
        return (self.nsem - 1, s)

    def _need(self, eng, ev, need):
        if ev is None:
            return
        (sid, sem), val, peng = ev
        if peng == "tensor" and eng == "tensor":
            return
        if need.get(sid, (None, 0))[1] < val:
            need[sid] = (sem, val)

    def _sync(self, eng, reads, writes):
        need = {}
        rr = [_region(a) for a in reads if str(a.space) != "PSUM"]
        ww = [_region(a) for a in writes] + [_region(a) for a in reads if str(a.space) == "PSUM"]
        for (nm, lo, hi) in rr:
            for (l2, h2, ev) in self.Wr.get(nm, ()):
                if l2 <= hi and lo <= h2:
                    self._need(eng, ev, need)
        for (nm, lo, hi) in ww:
            for (l2, h2, ev) in self.Wr.get(nm, ()):
                if l2 <= hi and lo <= h2:
                    self._need(eng, ev, need)
            for (l2, h2, ev) in self.Rd.get(nm, ()):
                if l2 <= hi and lo <= h2:
                    self._need(eng, ev, need)
        e = getattr(self.nc, eng)
        wd = self.waited[eng]
        for sid, (sem, val) in need.items():
            if wd.get(sid, 0) < val:
                e.wait_ge(sem, val)
                wd[sid] = val
        self.pending = (rr, ww)

    def _record(self, ev):
        rr, ww = self.pending
        for (nm, lo, hi) in ww:
            wl = [x for x in self.Wr.get(nm, []) if not (lo <= x[0] and x[1] <= hi)]
            wl.append((lo, hi, ev))
            self.Wr[nm] = wl
            self.Rd[nm] = [x for x in self.Rd.get(nm, []) if not (lo <= x[0] and x[1] <= hi)]
        for (nm, lo, hi) in rr:
            if nm in self.readonly:
                continue
            rl = [x for x in self.Rd.get(nm, []) if not (x[0] == lo and x[1] == hi and x[2][0][0] == ev[0][0])]
            rl.append((lo, hi, ev))
            self.Rd[nm] = rl
        self.pending = None
        self.last = ev

    def emit(self, eng, fn, dma=False, chain=False):
        if dma:
            return self._emit_dma(eng, fn)
        if eng not in self.sem or self.cnt[eng] >= 30000:
            self.sem[eng] = self._newsem()
            self.cnt[eng] = 0
        ins = fn(_Proxy(self, eng))
        self.cnt[eng] += 1
        ins.then_inc(self.sem[eng][1], 1)
        self._record((self.sem[eng], self.cnt[eng], eng))

    def _emit_dma(self, q, fn):
        if q not in self.dpool:
            self.dpool[q] = [self._newsem() for _ in range(self.NDMA)]
            self.dcnt[q] = [0] * self.NDMA
            self.dk[q] = 0
        slot = self.dk[q] % self.NDMA
        self.dk[q] += 1
        if self.dcnt[q][slot] >= 30000:
            self.dpool[q][slot] = self._newsem()
            self.dcnt[q][slot] = 0
        sid, sem = self.dpool[q][slot]
        if self.dcnt[q][slot] > 0 and self.waited[q].get(sid, 0) < self.dcnt[q][slot]:
            getattr(self.nc, q).wait_ge(sem, self.dcnt[q][slot])
            self.waited[q][sid] = self.dcnt[q][slot]
        ins = fn(_Proxy(self, q))
        self.dcnt[q][slot] += 16
        ins.then_inc(sem, 16)
        self._record(((sid, sem), self.dcnt[q][slot], "dma"))

    def dma(self, out, in_, q="sync", **kw):
        self.emit(q, lambda e: e.dma_start(out=out, in_=in_, **kw), dma=True)

    def mm(self, out, lhsT, rhs, start=True, stop=True):
        self.emit("tensor", lambda e: e.matmul(out, lhsT, rhs, start=start, stop=stop))

    def act(self, out, in_, func, **kw):
        self.emit("scalar", lambda e: e.activation(out=out, in_=in_, func=func, **kw))

    def V(self, fn):
        self.emit("vector", fn)

    def G(self, fn):
        self.emit("gpsimd", fn)

    def barrier(self):
        evs = [(sid, sem, self.cnt[eng]) for eng, (sid, sem) in self.sem.items()]
        for q in self.dpool:
            evs += [(sid, sem, c) for (sid, sem), c in zip(self.dpool[q], self.dcnt[q]) if c]
        for eng in self.ENG:
            e = getattr(self.nc, eng)
            wd = self.waited[eng]
            for sid, sem, val in evs:
                if wd.get(sid, 0) < val:
                    e.wait_ge(sem, val)
                    wd[sid] = val

    def phase(self):
        prog = self

        class _Ph(ExitStack):
            def __exit__(self, *a):
                prog.barrier()
                return super().__exit__(*a)
        return _Ph()

    def finish(self):
        e = self.nc.sync
        for eng, (sid, sem) in self.sem.items():
            e.wait_ge(sem, self.cnt[eng])
        for q in self.dpool:
            for (sid, sem), c in zip(self.dpool[q], self.dcnt[q]):
                if c:
                    e.wait_ge(sem, c)


def bc(ap, axis, n):
    a = ap.unsqueeze(axis)
    shp = list(a.shape)
    shp[axis] = n
    return a.broadcast_to(shp)


def build(layers, n_ctx_blk, n_lat_blk, off, R, gather, final_norm=True, n_cores=8):
    NB = n_ctx_blk + n_lat_blk
    NT = NB * 128
    nc = bass.Bass("TRN2", target_bir_lowering=False)
    x_in = nc.dram_tensor("x_in", [NT, D], F32, kind="ExternalInput").ap()
    cvec = nc.dram_tensor("cvec", [2, D], F32, kind="ExternalInput").ap()
    y_out = nc.dram_tensor("y_out", [n_lat_blk * 128, D], F32, kind="ExternalOutput").ap()
    if gather:
        wsh = nc.dram_tensor("wsh", [R // n_cores, D], F32, kind="ExternalInput").ap()
        wsrc = nc.dram_tensor("wsrc", [R // n_cores, D], F32).ap()
        W = nc.dram_tensor("wall", [R, D], F32).ap()
    else:
        W = nc.dram_tensor("wall", [R, D], F32, kind="ExternalInput").ap()
    X = nc.dram_tensor("X", [NT, D], F32).ap()
    ub16 = nc.dram_tensor("ub16", [16384, D], BF16).ap()
    vb16 = nc.dram_tensor("vb16", [16384, D], BF16).ap()
    S_d = nc.dram_tensor("S_d", [NT, 2048], F32).ap()
    ST_d = nc.dram_tensor("ST_d", [NT, 16], F32).ap()
    HT_d = nc.dram_tensor("HT_d", [NB * 128, D], BF16).ap()
    has_ssm = any(i % 2 == 1 for i in layers)
    if has_ssm:
        UT_d = [nc.dram_tensor(f"UT_d{d}", [NB * 128, D], BF16).ap() for d in range(2)]
        U_d = nc.dram_tensor("U_d", [NT, D], F32).ap()
        Y_d = [nc.dram_tensor(f"Y_d{d}", [NT, D], F32).ap() for d in range(2)]

    with ExitStack() as top:
        P = Prog(nc, top, readonly=("wall", "x_in", "cvec"))
        _ctr = [0]

        def sb(name, shape, dt, st=top):
            _ctr[0] += 1
            return st.enter_context(nc.sbuf_tensor(f"{name}_{_ctr[0]}", shape, dt))
        ps = [top.enter_context(nc.psum_tensor(f"psum{j}", [128, 1024], F32)) for j in range(4)]

        ident = sb("ident", [128, 128], F32)
        jrev = sb("jrev", [128, 128], F32)
        identb = sb("identb", [128, 128], BF16)
        ones = sb("ones", [128, 128], F32)
        P.G(lambda e: e.memset(ones[:], 1.0))
        P.G(lambda e: e.affine_select(out=ident[:], in_=ones[:], pattern=[[1, 128]], compare_op=ALU.is_equal,
                                      fill=0.0, base=0, channel_multiplier=-1))
        P.G(lambda e: e.affine_select(out=jrev[:], in_=ones[:], pattern=[[1, 128]], compare_op=ALU.is_equal,
                                      fill=0.0, base=-127, channel_multiplier=1))
        P.V(lambda e: e.tensor_copy(out=identb[:], in_=ident[:]))

        if gather:
            P.dma(wsrc[:, :], wsh[:, :])
            P.emit("gpsimd", lambda e: e.collective_compute(
                "AllGather", ALU.bypass, ins=[wsrc[:, :]], outs=[W[:, :]],
                replica_groups=[list(range(n_cores))]), dma=True)

        def is_ctx(b):
            return b < n_ctx_blk

        xt = sb("xt", [128, D], F32)
        xs = sb("xs", [128, D], F32)
        tmp = sb("tmp", [128, D], F32)
        hT = sb("hT", [128, 8, 128], BF16)
        st1 = sb("st1", [128, 8], F32)
        scT = sb("scT", [128, 8, 2], F32)
        modT = sb("modT", [128, 48, 2], F32)
        gB = sb("gB", [128, 2, 2, D], F32)
        gwT = sb("gwT", [128, 2, 8, 2], F32)

        for v in range(2):
            P.dma(scT[:, :, v], cvec[v:v + 1, :].rearrange("o (k p) -> p (o k)", p=128), allow_slow_non_contiguous=True)
        P.act(scT[:], scT[:], AF.Silu)

        for b in range(NB):
            P.dma(xt[:], x_in[b * 128:(b + 1) * 128, :])
            if not is_ctx(b):
                lb = b - n_ctx_blk
                P.dma(xs[:], W[off["pos"] + lb * 128: off["pos"] + (lb + 1) * 128, :])
                P.V(lambda e: e.tensor_add(out=xt[:], in0=xt[:], in1=xs[:]))
            P.dma(X[b * 128:(b + 1) * 128, :], xt[:])

        def load_cast(dst_bf, src_rows, stg):
            P.dma(stg, src_rows.rearrange("(k p) c -> p k c", p=128))
            P.act(dst_bf, stg, AF.Copy)

        def modulation(i, stg, ls):
            scR = sb("scR", [128, 2, 8, 128], F32, ls)
            ngT = sb("ngT", [128, 2, 8], F32, ls)
            abT = sb("abT", [128, 48], F32, ls)
            abB = sb("abB", [128, 2, D], F32, ls)
            for v in range(2):
                P.V(lambda e, v=v: e.tensor_copy(out=scR[:, v], in_=bc(scT[:, :, v], 2, 128)))
            P.dma(abT[:], W[off[f"ada_b{i}"]:off[f"ada_b{i}"] + 6, :].rearrange("s (k p) -> p (s k)", p=128),
                  allow_slow_non_contiguous=True)
            for w, seg in enumerate((2, 5)):
                P.dma(abB[:, w, :], W[off[f"ada_b{i}"] + seg: off[f"ada_b{i}"] + seg + 1, :].partition_broadcast(128))
            for nn, nm in enumerate((f"n1g{i}", f"n2g{i}")):
                P.dma(ngT[:, nn, :], W[off[nm]:off[nm] + 1, :].rearrange("o (k p) -> p (o k)", p=128),
                      allow_slow_non_contiguous=True)
            for seg in range(6):
                r0 = off[f"ada_w{i}"] + seg * D
                P.dma(stg, W[r0:r0 + D, :].rearrange("(k p) c -> p k c", p=128))
                for kc in range(8):
                    for kd in range(8):
                        P.mm(ps[0][:, (seg * 8 + kc) * 2:(seg * 8 + kc) * 2 + 2], stg[:, kd, kc * 128:(kc + 1) * 128],
                             scT[:, kd, :], start=(kd == 0), stop=(kd == 7))
                if seg in (2, 5):
                    w = 0 if seg == 2 else 1
                    for v in range(2):
                        for n in range(2):
                            for kd in range(8):
                                P.mm(ps[1 + v][:, n * 512:(n + 1) * 512], scR[:, v, kd, :],
                                     stg[:, kd, n * 512:(n + 1) * 512], start=(kd == 0), stop=(kd == 7))
                        P.V(lambda e, v=v, w=w: e.tensor_add(out=gB[:, w, v, :], in0=ps[1 + v][:], in1=abB[:, w, :]))
            P.V(lambda e: e.tensor_add(out=modT[:], in0=ps[0][:, 0:96].rearrange("p (s v) -> p s v", v=2),
                                       in1=bc(abT[:], 2, 2)))
            for nn, sseg in enumerate((1, 4)):
                P.V(lambda e, nn=nn, sseg=sseg: e.scalar_tensor_tensor(
                    out=gwT[:, nn], in0=modT[:, sseg * 8:(sseg + 1) * 8, :], scalar=1.0,
                    in1=bc(ngT[:, nn, :], 2, 2), op0=ALU.add, op1=ALU.mult))

        def norm_T(nn, v, dst, src=None, rev=False):
            src = xt if src is None else src
            sseg = 0 if nn == 0 else 3
            P.act(xs[:], src[:], AF.Square, accum_out=st1[:, 0:1])
            P.act(st1[:, 1:2], st1[:, 0:1], AF.Sqrt, scale=1.0 / D, bias=epsb[:, 0:1])
            P.V(lambda e: e.reciprocal(out=st1[:, 2:3], in_=st1[:, 1:2]))
            P.act(xs[:], src[:], AF.Copy, scale=st1[:, 2:3])
            pt = ps[0][:].rearrange("p (k t) -> p k t", t=128)
            for k in range(8):
                P.mm(pt[:, k, :], xs[:, k * 128:(k + 1) * 128], (jrev if rev else ident)[:])
            for k in range(8):
                P.act(dst[:, k, :], pt[:, k, :], AF.Identity, scale=gwT[:, nn, k, v:v + 1],
                      bias=modT[:, sseg * 8 + k, v:v + 1])

        epsb = sb("epsb", [128, 1], F32)
        P.G(lambda e: e.memset(epsb[:], EPS))

        for li, i in enumerate(layers):
            last = (i == 3)
            with P.phase() as ls:
                stg = sb(f"stg{i}", [128, 8, D], F32, ls)
                modulation(i, stg[:], ls)
            C = SimpleNamespace(**{k: v for k, v in locals().items() if not k.startswith('_')})
            if i % 2 == 0:
                chunk_layer(C, i)
            else:
                ssm_layer(C, i, last)
            C = SimpleNamespace(**{k: v for k, v in locals().items() if not k.startswith('_')})
            peer_layer(C, i, last)

        fg = sb("fg", [128, D], F32)
        P.dma(fg[:], W[off["final_g"]:off["final_g"] + 1, :].partition_broadcast(128))
        for b in range(n_ctx_blk, NB):
            P.dma(xt[:], X[b * 128:(b + 1) * 128, :])
            if final_norm:
                P.act(xs[:], xt[:], AF.Square, accum_out=st1[:, 0:1])
                P.act(st1[:, 1:2], st1[:, 0:1], AF.Sqrt, scale=1.0 / D, bias=epsb[:, 0:1])
                P.V(lambda e: e.reciprocal(out=st1[:, 2:3], in_=st1[:, 1:2]))
                P.V(lambda e: e.scalar_tensor_tensor(out=xs[:], in0=xt[:], scalar=st1[:, 2:3], in1=fg[:],
                                                     op0=ALU.mult, op1=ALU.mult))
                P.dma(y_out[(b - n_ctx_blk) * 128:(b - n_ctx_blk + 1) * 128, :], xs[:])
            else:
                P.dma(y_out[(b - n_ctx_blk) * 128:(b - n_ctx_blk + 1) * 128, :], xt[:])
        P.finish()
    return nc


def chunk_layer(C, i):
    nc, P, sb, ps, off, W, X, NB, is_ctx = C.nc, C.P, C.sb, C.ps, C.off, C.W, C.X, C.NB, C.is_ctx
    xt, xs, tmp, hT, gB, norm_T, load_cast, epsb = C.xt, C.xs, C.tmp, C.hT, C.gB, C.norm_T, C.load_cast, C.epsb
    with P.phase() as cs:
        stg = sb("cstg", [128, 8, D], F32, cs)
        w_in = sb("cw_in", [128, 8, 2048], BF16, cs)
        w_out = sb("cw_out", [128, 8, D], BF16, cs)
        wsT = sb("cwsT", [128, 8, 128], BF16, cs)
        wsTf = sb("cwsTf", [128, 8, 128], F32, cs)
        lng = sb("clng", [128, D], F32, cs)
        lnb = sb("clnb", [128, D], F32, cs)
        bsb = sb("cbsb", [128, D], F32, cs)
        uT = sb("cuT", [128, D], F32, cs)
        vv = sb("cvv", [128, D], F32, cs)
        vln = sb("cvln", [128, D], BF16, cs)
        mT = sb("cmT", [128, 8, 128], BF16, cs)
        bst = sb("cbst", [128, 2, 6], F32, cs)
        mv = sb("cmv", [128, 4], F32, cs)
        o = off[f"cm_w_in{i}"]
        for m in range(2):
            load_cast(w_in[:, :, m * D:(m + 1) * D], W[o + m * D:o + (m + 1) * D, :], stg[:])
        o = off[f"cm_w_out{i}"]
        load_cast(w_out[:], W[o:o + D, :], stg[:])
        o = off[f"cm_wsT{i}"]
        P.dma(wsTf[:], W[o:o + 128, :].rearrange("r (a p) -> (r a) p", p=128).rearrange("(h q) p -> q h p", q=128))
        P.V(lambda e: e.tensor_copy(out=wsT[:], in_=wsTf[:]))
        for t_, nm in ((lng, "cm_ln_g"), (lnb, "cm_ln_b"), (bsb, "cm_bs")):
            o = off[f"{nm}{i}"]
            P.dma(t_[:], W[o:o + 1, :].partition_broadcast(128))
        xts = [xt, sb("cxt2", [128, D], F32, cs)]
        uTs = [uT, sb("cuT2", [128, D], F32, cs)]
        vvs = [vv, sb("cvv2", [128, D], F32, cs)]

        def front(b):
            v = 1 if is_ctx(b) else 0
            xb, uT_, vv_ = xts[b % 2], uTs[b % 2], vvs[b % 2]
            P.dma(xb[:], X[b * 128:(b + 1) * 128, :])
            norm_T(0, v, hT, src=xb)
            pu = ps[1][:].rearrange("p (f t) -> p f t", t=128)
            for f in range(8):
                for kd in range(8):
                    P.mm(pu[:, f, :], w_in[:, kd, f * 128:(f + 1) * 128], hT[:, kd, :], start=(kd == 0), stop=(kd == 7))
            for n in range(2):
                P.act(uT_[:, n * 512:(n + 1) * 512], ps[1][:, n * 512:(n + 1) * 512], AF.Gelu)
            for n in range(2):
                for kd in range(8):
                    P.mm(ps[2][:, n * 512:(n + 1) * 512], hT[:, kd, :], w_in[:, kd, D + n * 512:D + (n + 1) * 512],
                         start=(kd == 0), stop=(kd == 7))
            for n in range(2):
                P.act(vv_[:, n * 512:(n + 1) * 512], ps[2][:, n * 512:(n + 1) * 512], AF.Gelu)

        def ln_sp(b):
            vv_ = vvs[b % 2]
            for n in range(2):
                P.V(lambda e, n=n: e.bn_stats(out=bst[:, n, :], in_=vv_[:, n * 512:(n + 1) * 512]))
            P.V(lambda e: e.bn_aggr(out=mv[:, 0:2], in_=bst[:]))
            P.act(mv[:, 2:3], mv[:, 1:2], AF.Sqrt, scale=1.0, bias=epsb[:, 0:1])
            P.V(lambda e: e.reciprocal(out=mv[:, 3:4], in_=mv[:, 2:3]))
            P.V(lambda e: e.tensor_scalar(out=vv_[:], in0=vv_[:], scalar1=mv[:, 0:1], scalar2=mv[:, 3:4],
                                          op0=ALU.subtract, op1=ALU.mult))
            P.V(lambda e: e.tensor_mul(out=vv_[:], in0=vv_[:], in1=lng[:]))
            P.V(lambda e: e.tensor_add(out=vln[:], in0=vv_[:], in1=lnb[:]))
            pss = ps[3][:].rearrange("p (h t) -> p h t", t=128)
            for h in range(8):
                P.mm(pss[:, h, :], vln[:, h * 128:(h + 1) * 128], wsT[:, h, :])

        def back(b):
            v = 1 if is_ctx(b) else 0
            xb, uT_ = xts[b % 2], uTs[b % 2]
            P.V(lambda e: e.tensor_add(out=tmp[:], in0=ps[3][:], in1=bsb[:]))
            P.V(lambda e: e.tensor_mul(out=mT[:].rearrange("p h t -> p (h t)"), in0=tmp[:], in1=uT_[:]))
            for n in range(2):
                for h in range(8):
                    P.mm(ps[3][:, n * 512:(n + 1) * 512], mT[:, h, :], w_out[:, h, n * 512:(n + 1) * 512],
                         start=(h == 0), stop=(h == 7))
            P.V(lambda e: e.tensor_mul(out=tmp[:], in0=ps[3][:], in1=gB[:, 0, v, :]))
            P.V(lambda e: e.tensor_add(out=xb[:], in0=xb[:], in1=tmp[:]))
            P.dma(X[b * 128:(b + 1) * 128, :], xb[:])

        front(0)
        for b in range(NB):
            ln_sp(b)
            if b + 1 < NB:
                front(b + 1)
            back(b)


def peer_layer(C, i, last):
    nc, P, sb, ps, off, W, X, NB, is_ctx = C.nc, C.P, C.sb, C.ps, C.off, C.W, C.X, C.NB, C.is_ctx
    xt, xs, tmp, hT, gB, norm_T, load_cast = C.xt, C.xs, C.tmp, C.hT, C.gB, C.norm_T, C.load_cast
    ub16, vb16, S_d, ST_d, HT_d, identb = C.ub16, C.vb16, C.S_d, C.ST_d, C.HT_d, C.identb
    blocks = [b for b in range(NB) if not (last and is_ctx(b))]
    with P.phase() as c0:
        stgs = [sb("pstg", [128, 8, D], F32, c0) for _ in range(2)]
        stbs = [sb("pstb", [128, 8, D], BF16, c0) for _ in range(2)]
        k_ = 0
        for name, dst in ((f"uT{i}", ub16), (f"v{i}", vb16)):
            for sc in range(16):
                o = off[name] + sc * D
                P.dma(stgs[k_ % 2][:], W[o:o + D, :].rearrange("(k p) c -> p k c", p=128))
                P.act(stbs[k_ % 2][:], stgs[k_ % 2][:], AF.Copy)
                P.dma(dst[sc * D:(sc + 1) * D, :].rearrange("(k p) c -> p k c", p=128), stbs[k_ % 2][:])
                k_ += 1
    with P.phase() as cs:
        Weff = sb("pWeff", [128, 8, 2048], BF16, cs)
        with P.phase() as c1:
            stg = sb("pstg1", [128, 8, D], F32, c1)
            wqT = sb("pwqT", [128, 32, D], BF16, c1)
            keysT = sb("pkeysT", [128, 32, 128], BF16, c1)
            for m in range(4):
                o = off[f"wqT{i}"] + m * D
                load_cast(wqT[:, m * 8:(m + 1) * 8, :], W[o:o + D, :], stg[:])
            o = off[f"keysT{i}"]
            kf = stg[:].rearrange("p k c -> p (k c)")[:, 0:4096].rearrange("p (a m) -> p a m", m=128)
            P.dma(kf, W[o:o + 512, :].rearrange("r (a m) -> (r a) m", m=128).rearrange("(a e) m -> e a m", e=128))
            P.act(keysT[:], kf, AF.Copy)
            for dk in range(8):
                for hq in range(4):
                    bank = ps[1 + (hq % 2)][:, 0:512]
                    pw = bank.rearrange("p (a m) -> p a m", m=128)
                    for a_ in range(4):
                        hk = hq * 4 + a_
                        for j in range(2):
                            P.mm(pw[:, a_, :], wqT[:, hk * 2 + j, dk * 128:(dk + 1) * 128], keysT[:, hk * 2 + j, :],
                                 start=(j == 0), stop=(j == 1))
                    P.act(Weff[:, dk, hq * 512:(hq + 1) * 512], bank, AF.Copy)
        ss = [sb("ps_", [128, 16, 128], F32, cs) for _ in range(2)]
        hTs = [sb("phT1", [128, 8, 128], BF16, cs) for _ in range(2)]
        s2 = sb("ps2", [128, 128], F32, cs)
        T16 = sb("pT16", [128, 16, 16], F32, cs)
        cand = sb("pcand", [128, 8, 256], F32, cs)
        cand2 = sb("pcand2", [128, 256], F32, cs)
        TS = sb("pTS", [128, 8, 16], F32, cs)
        ex = sb("pex", [128, 8, 16], F32, cs)
        stat = sb("pstat", [128, 16], F32, cs)
        zz = sb("pzz", [128, 8], F32, cs)

        def front(bi):
            b = blocks[bi]
            hT_, s = hTs[bi % 2], ss[bi % 2]
            v = 1 if is_ctx(b) else 0
            P.dma(xt[:], X[b * 128:(b + 1) * 128, :])
            norm_T(1, v, hT_)
            P.dma(HT_d[b * 128:(b + 1) * 128, :].rearrange("p (k t) -> p k t", t=128), hT_[:])
            for half in range(2):
                for n in range(2):
                    col = half * 1024 + n * 512
                    for kd in range(8):
                        P.mm(ps[1 + half][:, n * 512:(n + 1) * 512], hT_[:, kd, :], Weff[:, kd, col:col + 512],
                             start=(kd == 0), stop=(kd == 7))
                    P.act(s[:, half * 8 + n * 4:half * 8 + n * 4 + 4, :],
                          ps[1 + half][:, n * 512:(n + 1) * 512].rearrange("p (a m) -> p a m", m=128), AF.Copy)
            P.dma(S_d[b * 128:(b + 1) * 128, :], s[:].rearrange("p a m -> p (a m)"))

        def back(bi):
            b = blocks[bi]
            s = ss[bi % 2]
            for hk in range(16):
                P.V(lambda e, hk=hk: e.max(out=T16[:, hk, 0:8], in_=s[:, hk, :]))
                P.V(lambda e, hk=hk: e.match_replace(out=s2[:], in_to_replace=T16[:, hk, 0:8], in_values=s[:, hk, :],
                                                     imm_value=NEG))
                P.V(lambda e, hk=hk: e.max(out=T16[:, hk, 8:16], in_=s2[:]))
            T4 = T16[:].rearrange("p (h k) a -> p h k a", k=2)
            P.V(lambda e: e.tensor_tensor(out=cand[:].rearrange("p h (a b) -> p h a b", b=16),
                                          in0=bc(T4[:, :, 0, :], 3, 16), in1=bc(T4[:, :, 1, :], 2, 16), op=ALU.add))
            for h in range(8):
                P.V(lambda e, h=h: e.max(out=TS[:, h, 0:8], in_=cand[:, h, :]))
                P.V(lambda e, h=h: e.match_replace(out=cand2[:], in_to_replace=TS[:, h, 0:8], in_values=cand[:, h, :],
                                                   imm_value=NEG))
                P.V(lambda e, h=h: e.max(out=TS[:, h, 8:16], in_=cand2[:]))
            P.V(lambda e: e.tensor_tensor(out=ex[:], in0=TS[:], in1=bc(TS[:, :, 0], 2, 16), op=ALU.subtract))
            P.act(ex[:], ex[:], AF.Exp)
            P.V(lambda e: e.tensor_reduce(out=zz[:], in_=ex[:], axis=AX.X, op=ALU.add))
            P.act(zz[:], zz[:], AF.Ln)
            P.V(lambda e: e.tensor_copy(out=stat[:, 0:8], in_=TS[:, :, 15]))
            P.V(lambda e: e.scalar_tensor_tensor(out=stat[:, 8:16], in0=TS[:, :, 0], scalar=-1.0, in1=zz[:],
                                                 op0=ALU.mult, op1=ALU.subtract))
            P.dma(ST_d[b * 128:(b + 1) * 128, :], stat[:])

        front(0)
        for bi in range(len(blocks)):
            if bi + 1 < len(blocks):
                front(bi + 1)
            back(bi)
    DELTA = 1e-5
    GBK = 2
    HP = 7
    with P.phase() as cs:
        UTc = [sb("pUTc", [128, 8, 512], BF16, cs) for _ in range(3)]
        Vc = [sb("pVc", [128, 4, D], BF16, cs) for _ in range(3)]
        sS = [sb("psS", [128, 16, 128], F32, cs) for _ in range(GBK)]
        statb = [sb("pstatb", [128, 16], F32, cs) for _ in range(GBK)]
        cst = [sb("pcst", [128, 8], F32, cs) for _ in range(GBK)]
        hTg = [sb("phTg", [128, 8, 128], BF16, cs) for _ in range(GBK)]
        rr_ = [sb("prr", [128, 8, 512], F32, cs) for _ in range(2)]
        wv = [sb("pwv", [128, 8, 512], BF16, cs) for _ in range(2)]
        mk = [sb("pmk", [128, 8, 512], BF16, cs) for _ in range(2)]
        t4 = sb("pt4", [128, 4, 512], BF16, cs)
        t2 = sb("pt2", [128, 2, 512], BF16, cs)
        Gs = [sb("pGs", [128, 512], BF16, cs) for _ in range(2)]
        ad = [sb("pad", [128, 512], BF16, cs) for _ in range(2)]
        A = [sb("pA", [128, 512], BF16, cs) for _ in range(2)]
        AT = [sb("pAT", [128, 4, 128], BF16, cs) for _ in range(2)]
        psd = [ps[0][:, 0:512], ps[0][:, 512:1024]]
        psT = [ps[1][:, 0:512], ps[1][:, 512:1024]]
        pso = [ps[2], ps[3]]
        xg = [xt, xs]
        groups = [blocks[k:k + GBK] for k in range(0, len(blocks), GBK)]
        for grp in groups:
            ng = len(grp)
            for gi, b in enumerate(grp):
                P.dma(hTg[gi][:], HT_d[b * 128:(b + 1) * 128, :].rearrange("p (k t) -> p k t", t=128))
                P.dma(sS[gi][:].rearrange("p a m -> p (a m)"), S_d[b * 128:(b + 1) * 128, :])
                P.dma(statb[gi][:], ST_d[b * 128:(b + 1) * 128, :])
                P.dma(xg[gi][:], X[b * 128:(b + 1) * 128, :])
                s4 = sS[gi][:].rearrange("p (h k) m -> p h k m", k=2)
                P.V(lambda e, gi=gi, s4=s4: e.tensor_tensor(out=s4[:, :, 0, :], in0=s4[:, :, 0, :],
                                                            in1=bc(statb[gi][:, 0:8], 2, 128), op=ALU.subtract))
                P.V(lambda e, s4=s4: e.tensor_scalar_add(out=s4[:, :, 0, :], in0=s4[:, :, 0, :], scalar1=DELTA))
                P.V(lambda e, gi=gi: e.tensor_tensor(out=cst[gi][:], in0=statb[gi][:, 0:8], in1=statb[gi][:, 8:16],
                                                     op=ALU.add))
                P.V(lambda e, gi=gi: e.tensor_scalar_add(out=cst[gi][:], in0=cst[gi][:], scalar1=-DELTA))
            chunks = [(nch, gi) for nch in range(32) for gi in range(ng)]
            N = len(chunks)

            def T(nch):
                sc, half = nch // 2, nch % 2
                P.dma(UTc[nch % 3][:], ub16[sc * D:(sc + 1) * D, half * 512:(half + 1) * 512].rearrange(
                    "(k p) c -> p k c", p=128))
                r0 = nch * 512
                P.dma(Vc[nch % 3][:], vb16[r0:r0 + 512, :].rearrange("(c j) d -> j c d", j=128))

            def R(c):
                nch, gi = chunks[c]
                p = c % 2
                i0 = nch * 4
                s4 = sS[gi][:].rearrange("p (h k) m -> p h k m", k=2)
                rv = rr_[p][:].rearrange("p h (i j) -> p h i j", j=128)
                P.V(lambda e: e.tensor_tensor(out=rv, in0=bc(s4[:, :, 0, i0:i0 + 4], 3, 128),
                                              in1=bc(s4[:, :, 1, :], 2, 4), op=ALU.add))

            def Wst(c):
                nch, gi = chunks[c]
                p = c % 2
                for h in range(8):
                    P.act(wv[p][:, h, :], rr_[p][:, h, :], AF.Exp, bias=cst[gi][:, h:h + 1], scale=1.0)

            def M(c):
                p = c % 2
                P.V(lambda e: e.scalar_tensor_tensor(out=mk[p][:].rearrange("p h x -> p (h x)"),
                                                     in0=rr_[p][:].rearrange("p h x -> p (h x)"), scalar=0.0,
                                                     in1=wv[p][:].rearrange("p h x -> p (h x)"),
                                                     op0=ALU.is_ge, op1=ALU.mult))
                P.V(lambda e: e.tensor_add(out=t4[:], in0=mk[p][:, 0:4, :], in1=mk[p][:, 4:8, :]))
                P.V(lambda e: e.tensor_add(out=t2[:], in0=t4[:, 0:2, :], in1=t4[:, 2:4, :]))
                P.V(lambda e: e.tensor_add(out=Gs[p][:], in0=t2[:, 0, :], in1=t2[:, 1, :]))

            def S1(c):
                nch, gi = chunks[c]
                p = c % 2
                for kd in range(8):
                    P.mm(psd[p], hTg[gi][:, kd, :], UTc[nch % 3][:, kd, :], start=(kd == 0), stop=(kd == 7))

            def S2(c):
                p = c % 2
                P.act(ad[p][:], psd[p], AF.Gelu)

            def S3(c):
                p = c % 2
                P.V(lambda e: e.tensor_mul(out=A[p][:], in0=ad[p][:], in1=Gs[p][:]))

            def S456(c):
                nch, gi = chunks[c]
                p = c % 2
                pT = psT[p].rearrange("p (c t) -> p c t", t=128)
                for cc in range(4):
                    P.mm(pT[:, cc, :], A[p][:, cc * 128:(cc + 1) * 128], identb[:])
                P.act(AT[p][:], pT, AF.Copy)
                for cc in range(4):
                    for nn in range(2):
                        P.mm(pso[gi][:, nn * 512:(nn + 1) * 512], AT[p][:, cc, :],
                             Vc[nch % 3][:, cc, nn * 512:(nn + 1) * 512],
                             start=(nch == 0 and cc == 0), stop=(nch == 31 and cc == 3))

            T(0)
            T(1)
            R(0)
            Wst(0)
            S1(0)
            S2(0)
            for c in range(N):
                nch, gi = chunks[c]
                if gi == 0 and nch + 2 < 32:
                    T(nch + 2)
                if c + 1 < N:
                    R(c + 1)
                    Wst(c + 1)
                    S1(c + 1)
                    S2(c + 1)
                M(c)
                S3(c)
                S456(c)
            for gi, b in enumerate(grp):
                v = 1 if is_ctx(b) else 0
                P.V(lambda e, gi=gi, v=v: e.tensor_mul(out=tmp[:], in0=pso[gi][:], in1=gB[:, 1, v, :]))
                P.V(lambda e, gi=gi: e.tensor_add(out=xg[gi][:], in0=xg[gi][:], in1=tmp[:]))
                P.dma(X[b * 128:(b + 1) * 128, :], xg[gi][:])


def ssm_layer(C, i, last):
    nc, P, sb, ps, off, W, X, NB, is_ctx = C.nc, C.P, C.sb, C.ps, C.off, C.W, C.X, C.NB, C.is_ctx
    xt, xs, tmp, hT, gB, norm_T, load_cast = C.xt, C.xs, C.tmp, C.hT, C.gB, C.norm_T, C.load_cast
    UT_d, U_d, Y_d, ident, jrev, n_ctx_blk = C.UT_d, C.U_d, C.Y_d, C.ident, C.jrev, C.n_ctx_blk

    def mirror(b):
        return (n_ctx_blk - 1 - b) if is_ctx(b) else (n_ctx_blk + (NB - 1 - b))

    with P.phase() as cs:
        stg = sb("sstg", [128, 8, D], F32, cs)
        w_in = sb("sw_in", [128, 8, D], BF16, cs)
        hTr = sb("shTr", [128, 8, 128], BF16, cs)
        uTb = sb("suTb", [128, 8, 128], BF16, cs)
        o = off[f"ssm_w_in{i}"]
        load_cast(w_in[:], W[o:o + D, :], stg[:])
        for b in range(NB):
            v = 1 if is_ctx(b) else 0
            P.dma(xt[:], X[b * 128:(b + 1) * 128, :])
            norm_T(0, v, hT)
            norm_T(0, v, hTr, rev=True)
            for n in range(2):
                for kd in range(8):
                    P.mm(ps[1][:, n * 512:(n + 1) * 512], hT[:, kd, :], w_in[:, kd, n * 512:(n + 1) * 512],
                         start=(kd == 0), stop=(kd == 7))
            P.V(lambda e: e.tensor_copy(out=tmp[:], in_=ps[1][:]))
            P.dma(U_d[b * 128:(b + 1) * 128, :], tmp[:])
            for d, hsrc, pos in ((0, hT, b), (1, hTr, mirror(b))):
                pu = ps[2][:].rearrange("p (f t) -> p f t", t=128)
                for f in range(8):
                    for kd in range(8):
                        P.mm(pu[:, f, :], w_in[:, kd, f * 128:(f + 1) * 128], hsrc[:, kd, :], start=(kd == 0), stop=(kd == 7))
                P.V(lambda e, pu=pu: e.tensor_copy(out=uTb[:], in_=pu))
                P.dma(UT_d[d][pos * 128:(pos + 1) * 128, :].rearrange("p (k t) -> p k t", t=128), uTb[:])

    with P.phase() as cs:
        Wr = sb("sWr", [128, 32, 128], F32, cs)
        Wi = sb("sWi", [128, 32, 128], F32, cs)
        Rr = sb("sRr", [128, 32, 128], F32, cs)
        Ri = sb("sRi", [128, 32, 128], F32, cs)
        Rho0 = sb("sRho0", [128, 32, 128], F32, cs)
        BTr = sb("sBTr", [128, 8, 4, 128], BF16, cs)
        BTi = sb("sBTi", [128, 8, 4, 128], BF16, cs)
        CTr = sb("sCTr", [128, 32, 32], BF16, cs)
        CTi = sb("sCTi", [128, 32, 32], BF16, cs)
        big = sb("sbig", [128, 6, 1024], F32, cs)
        wk = [big[:, j, :] for j in range(6)]
        hb = [sb(f"shb{j}", [128, 1024], BF16, cs) for j in range(4)]
        CTrn = sb("sCTrn", [128, 32, 32], BF16, cs)
        uTs = [sb("suT", [128, 8, 128], BF16, cs) for _ in range(2)]
        sm = [sb(f"ssm{j}", [128, 32], F32, cs) for j in range(12)]
        carS = [sb(f"scar{j}", [128, 32], F32, cs) for j in range(2)]
        ti32 = sb("sti32", [128, 32], mybir.dt.int32, cs)
        lr, li_, dl, rho, th, cs_, sn_, cr, ci, t0, t1_, t2_ = sm
        for d in range(2):
            with P.phase() as ss:
                stg = big[:].rearrange("p a c -> p (a c)")[:, 0:4096]
                for name, dst in ((f"bT_r{i}", BTr), (f"bT_i{i}", BTi)):
                    o = off[name] + d * 512
                    P.dma(stg[:], W[o:o + 512, :].rearrange("(p r) c -> p (r c)", p=128))
                    P.act(dst[:].rearrange("p a b c -> p (a b c)"), stg[:], AF.Copy)
                for name, dst, scl in ((f"cT_r{i}", CTr, 1.0), (f"cT_r{i}", CTrn, -1.0), (f"cT_i{i}", CTi, -1.0)):
                    o = off[name] + d * 128
                    P.dma(stg[:, 0:1024], W[o:o + 128, :])
                    P.act(dst[:].rearrange("p a b -> p (a b)"), stg[:, 0:1024], AF.Copy, scale=scl)
                for name, dst in ((f"lamre{i}", lr), (f"lamim{i}", li_), (f"lstep{i}", dl)):
                    o = off[name] + d * 4
                    P.dma(dst[:], W[o:o + 4, :].rearrange("r (a g) -> (r a) g", g=32))
                P.act(dl[:], dl[:], AF.Exp)
                P.V(lambda e: e.tensor_mul(out=rho[:], in0=lr[:], in1=dl[:]))
                P.act(rho[:], rho[:], AF.Exp)
                P.V(lambda e: e.tensor_mul(out=th[:], in0=li_[:], in1=dl[:]))
                for dst, shift in ((sn_, 0.0), (cs_, 0.5 * PI)):
                    P.V(lambda e, shift=shift: e.tensor_scalar_add(out=t1_[:], in0=th[:], scalar1=shift))
                    P.V(lambda e: e.tensor_scalar_mul(out=ti32[:], in0=t1_[:], scalar1=1.0 / (2.0 * PI)))
                    P.V(lambda e: e.tensor_copy(out=t0[:], in_=ti32[:]))
                    P.V(lambda e: e.scalar_tensor_tensor(out=t1_[:], in0=t0[:], scalar=-2.0 * PI, in1=t1_[:],
                                                         op0=ALU.mult, op1=ALU.add))
                    P.V(lambda e: e.tensor_scalar(out=t0[:], in0=t1_[:], scalar1=PI, scalar2=-2.0 * PI,
                                                  op0=ALU.is_gt, op1=ALU.mult))
                    P.V(lambda e: e.tensor_add(out=t1_[:], in0=t1_[:], in1=t0[:]))
                    P.V(lambda e: e.tensor_scalar(out=t0[:], in0=t1_[:], scalar1=-PI, scalar2=2.0 * PI,
                                                  op0=ALU.is_lt, op1=ALU.mult))
                    P.V(lambda e: e.tensor_add(out=t1_[:], in0=t1_[:], in1=t0[:]))
                    P.act(dst[:], t1_[:], AF.Sin)
                P.V(lambda e: e.tensor_mul(out=t0[:], in0=rho[:], in1=cs_[:]))
                P.V(lambda e: e.tensor_scalar_add(out=t0[:], in0=t0[:], scalar1=-1.0))
                P.V(lambda e: e.tensor_mul(out=t1_[:], in0=rho[:], in1=sn_[:]))
                P.V(lambda e: e.tensor_mul(out=cr[:], in0=t0[:], in1=lr[:]))
                P.V(lambda e: e.tensor_mul(out=t2_[:], in0=t1_[:], in1=li_[:]))
                P.V(lambda e: e.tensor_add(out=cr[:], in0=cr[:], in1=t2_[:]))
                P.V(lambda e: e.tensor_mul(out=ci[:], in0=t1_[:], in1=lr[:]))
                P.V(lambda e: e.tensor_mul(out=t2_[:], in0=t0[:], in1=li_[:]))
                P.V(lambda e: e.tensor_sub(out=ci[:], in0=ci[:], in1=t2_[:]))
                P.V(lambda e: e.tensor_mul(out=t0[:], in0=lr[:], in1=lr[:]))
                P.V(lambda e: e.tensor_mul(out=t1_[:], in0=li_[:], in1=li_[:]))
                P.V(lambda e: e.tensor_add(out=t0[:], in0=t0[:], in1=t1_[:]))
                P.V(lambda e: e.reciprocal(out=t0[:], in_=t0[:]))
                P.V(lambda e: e.tensor_mul(out=cr[:], in0=cr[:], in1=t0[:]))
                P.V(lambda e: e.tensor_mul(out=ci[:], in0=ci[:], in1=t0[:]))
                A1 = stg[:, 0:2048].rearrange("p (g k) -> p g k", k=64)
                A2 = stg[:, 2048:4096].rearrange("p (g k) -> p g k", k=64)
                P.V(lambda e: e.tensor_copy(out=Wr[:, :, 0], in_=cs_[:]))
                P.V(lambda e: e.tensor_copy(out=Wi[:, :, 0], in_=sn_[:]))
                s_ = 1
                while s_ < 128:
                    mr, mi = bc(Wr[:, :, s_ - 1], 2, s_), bc(Wi[:, :, s_ - 1], 2, s_)
                    a1, a2 = A1[:, :, 0:s_], A2[:, :, 0:s_]
                    P.V(lambda e, a1=a1, mr=mr, s_=s_: e.tensor_mul(out=a1, in0=Wr[:, :, 0:s_], in1=mr))
                    P.V(lambda e, a2=a2, mi=mi, s_=s_: e.tensor_mul(out=a2, in0=Wi[:, :, 0:s_], in1=mi))
                    P.V(lambda e, a1=a1, a2=a2, s_=s_: e.tensor_sub(out=Wr[:, :, s_:2 * s_], in0=a1, in1=a2))
                    P.V(lambda e, a1=a1, mi=mi, s_=s_: e.tensor_mul(out=a1, in0=Wr[:, :, 0:s_], in1=mi))
                    P.V(lambda e, a2=a2, mr=mr, s_=s_: e.tensor_mul(out=a2, in0=Wi[:, :, 0:s_], in1=mr))
                    P.V(lambda e, a1=a1, a2=a2, s_=s_: e.tensor_add(out=Wi[:, :, s_:2 * s_], in0=a1, in1=a2))
                    s_ *= 2
                B1 = stg[:].rearrange("p (g k) -> p g k", k=128)
                P.V(lambda e: e.tensor_mul(out=Rr[:], in0=Wr[:], in1=bc(cr[:], 2, 128)))
                P.V(lambda e: e.tensor_mul(out=B1, in0=Wi[:], in1=bc(ci[:], 2, 128)))
                P.V(lambda e: e.tensor_add(out=Rr[:], in0=Rr[:], in1=B1))
                P.V(lambda e: e.tensor_mul(out=Ri[:], in0=Wr[:], in1=bc(ci[:], 2, 128)))
                P.V(lambda e: e.tensor_mul(out=B1, in0=Wi[:], in1=bc(cr[:], 2, 128)))
                P.V(lambda e: e.tensor_sub(out=Ri[:], in0=Ri[:], in1=B1))
                P.V(lambda e: e.tensor_copy(out=Rho0[:], in_=bc(rho[:], 2, 128)))
                P.V(lambda e: e.memset(Rho0[:, :, 0], 0.0))
                P.V(lambda e: e.memset(carS[0][:], 0.0))
                P.V(lambda e: e.memset(carS[1][:], 0.0))
            steps = [(pos, q) for pos in range(NB) for q in range(4)]

            def load_u(pos):
                P.dma(uTs[pos % 2][:], UT_d[d][pos * 128:(pos + 1) * 128, :].rearrange("p (k t) -> p k t", t=128))

            def inject(pos, q):
                uT = uTs[pos % 2]
                Xr, Xi = ps[0][:], ps[1][:]
                for g8 in range(8):
                    gp = q * 8 + g8
                    ck, g4 = gp // 4, gp % 4
                    P.mm(Xr[:, g8 * 128:(g8 + 1) * 128], BTr[:, ck, g4, :], uT[:, ck, :])
                    P.mm(Xi[:, g8 * 128:(g8 + 1) * 128], BTi[:, ck, g4, :], uT[:, ck, :])

            def rotate_in(pos, q):
                Xr, Xi = ps[0][:], ps[1][:]
                gs = slice(q * 8, (q + 1) * 8)
                Rqr = Rr[:, gs, :].rearrange("p g k -> p (g k)")
                Rqi = Ri[:, gs, :].rearrange("p g k -> p (g k)")
                Wqr = Wr[:, gs, :].rearrange("p g k -> p (g k)")
                Wqi = Wi[:, gs, :].rearrange("p g k -> p (g k)")
                Rhq = Rho0[:, gs, :].rearrange("p g k -> p (g k)")
                k1, k2, btr, bti, gr, gi = wk
                hr, hi = btr, bti
                P.V(lambda e: e.tensor_mul(out=k1, in0=Xr, in1=Rqr))
                P.V(lambda e: e.tensor_mul(out=k2, in0=Xi, in1=Rqi))
                P.V(lambda e: e.tensor_sub(out=btr, in0=k1, in1=k2))
                P.V(lambda e: e.tensor_mul(out=k1, in0=Xi, in1=Rqr))
                P.V(lambda e: e.tensor_mul(out=k2, in0=Xr, in1=Rqi))
                P.V(lambda e: e.tensor_add(out=bti, in0=k1, in1=k2))

            def rest(pos, q):
                gs = slice(q * 8, (q + 1) * 8)
                Wqr = Wr[:, gs, :].rearrange("p g k -> p (g k)")
                Wqi = Wi[:, gs, :].rearrange("p g k -> p (g k)")
                Rhq = Rho0[:, gs, :].rearrange("p g k -> p (g k)")
                k1, k2, btr, bti, gr, gi = wk
                hr, hi = btr, bti
                b3r = btr.rearrange("p (g k) -> p g k", k=128)
                b3i = bti.rearrange("p (g k) -> p g k", k=128)
                P.V(lambda e: e.tensor_add(out=b3r[:, :, 0], in0=b3r[:, :, 0], in1=carS[0][:, gs]))
                P.V(lambda e: e.tensor_add(out=b3i[:, :, 0], in0=b3i[:, :, 0], in1=carS[1][:, gs]))
                P.V(lambda e: e.tensor_tensor_scan(out=gr, data0=Rhq, data1=btr, initial=0.0,
                                                   op0=ALU.mult, op1=ALU.add))
                P.V(lambda e: e.tensor_tensor_scan(out=gi, data0=Rhq, data1=bti, initial=0.0,
                                                   op0=ALU.mult, op1=ALU.add))
                P.V(lambda e: e.tensor_mul(out=k1, in0=gr, in1=Wqr))
                P.V(lambda e: e.tensor_mul(out=k2, in0=gi, in1=Wqi))
                P.V(lambda e: e.tensor_mul(out=btr, in0=gi, in1=Wqr))
                P.V(lambda e: e.tensor_mul(out=bti, in0=gr, in1=Wqi))
                e1, e2, e3, e4 = (t.rearrange("p (g k) -> p g k", k=128)[:, :, 127] for t in (k1, k2, btr, bti))
                P.V(lambda e: e.tensor_sub(out=t0[:, 0:8], in0=e1, in1=e2))
                P.V(lambda e: e.tensor_mul(out=carS[0][:, gs], in0=t0[:, 0:8], in1=rho[:, gs]))
                P.V(lambda e: e.tensor_add(out=t1_[:, 0:8], in0=e3, in1=e4))
                P.V(lambda e: e.tensor_mul(out=carS[1][:, gs], in0=t1_[:, 0:8], in1=rho[:, gs]))
                for j_, t in enumerate((k1, k2, btr, bti)):
                    P.act(hb[j_][:], t, AF.Copy)

            def readout(pos, q):
                for g8 in range(8):
                    gp = q * 8 + g8
                    for j_, ct in enumerate((CTr, CTrn, CTi, CTi)):
                        P.mm(ps[2][:, gp * 32:(gp + 1) * 32], hb[j_][:, g8 * 128:(g8 + 1) * 128], ct[:, gp, :],
                             start=(j_ == 0), stop=(j_ == 3))

            load_u(0)
            inject(*steps[0])
            for k, (pos, q) in enumerate(steps):
                if q == 0 and pos + 1 < NB:
                    load_u(pos + 1)
                rotate_in(pos, q)
                if k + 1 < len(steps):
                    inject(*steps[k + 1])
                rest(pos, q)
                readout(pos, q)
                if q == 3:
                    P.V(lambda e: e.tensor_copy(out=tmp[:], in_=ps[2][:]))
                    P.dma(Y_d[d][pos * 128:(pos + 1) * 128, :], tmp[:])

    with P.phase() as cs:
        stg = sb("sstg3", [128, 8, D], F32, cs)
        w_out = sb("sw_out", [128, 8, 2048], BF16, cs)
        dB = sb("sdB", [128, D], F32, cs)
        a1 = sb("sa1", [128, D], F32, cs)
        a2 = sb("sa2", [128, D], F32, cs)
        gyT = sb("sgyT", [128, 8, 128], BF16, cs)
        o = off[f"ssm_w_out{i}"]
        for m in range(2):
            load_cast(w_out[:, :, m * D:(m + 1) * D], W[o + m * D:o + (m + 1) * D, :], stg[:])
        o = off[f"ssm_d{i}"]
        P.dma(dB[:], W[o:o + 1, :].partition_broadcast(128))
        for b in range(NB):
            if last and is_ctx(b):
                continue
            v = 1 if is_ctx(b) else 0
            P.dma(xt[:], X[b * 128:(b + 1) * 128, :])
            P.dma(a1[:], U_d[b * 128:(b + 1) * 128, :])
            P.dma(tmp[:], Y_d[0][b * 128:(b + 1) * 128, :])
            P.dma(a2[:], Y_d[1][mirror(b) * 128:(mirror(b) + 1) * 128, :])
            P.V(lambda e: e.tensor_mul(out=a1[:], in0=a1[:], in1=dB[:]))
            P.V(lambda e: e.tensor_add(out=a1[:], in0=a1[:], in1=tmp[:]))
            pt = ps[0][:].rearrange("p (k t) -> p k t", t=128)
            for k in range(8):
                P.mm(pt[:, k, :], a1[:, k * 128:(k + 1) * 128], ident[:], start=True, stop=False)
                P.mm(pt[:, k, :], a2[:, k * 128:(k + 1) * 128], jrev[:], start=False, stop=True)
            for n in range(2):
                P.act(gyT[:].rearrange("p k t -> p (k t)")[:, n * 512:(n + 1) * 512], ps[0][:, n * 512:(n + 1) * 512], AF.Gelu)
            for n in range(4):
                po = ps[1 + n // 2][:, (n % 2) * 512:(n % 2 + 1) * 512]
                for k in range(8):
                    P.mm(po, gyT[:, k, :], w_out[:, k, n * 512:(n + 1) * 512], start=(k == 0), stop=(k == 7))
            for n in range(2):
                P.act(tmp[:, n * 512:(n + 1) * 512], ps[2][:, n * 512:(n + 1) * 512], AF.Sigmoid)
            P.V(lambda e: e.tensor_mul(out=tmp[:], in0=ps[1][:], in1=tmp[:]))
            P.V(lambda e, v=v: e.tensor_mul(out=tmp[:], in0=tmp[:], in1=gB[:, 0, v, :]))
            P.V(lambda e: e.tensor_add(out=xt[:], in0=xt[:], in1=tmp[:]))
            P.dma(X[b * 128:(b + 1) * 128, :], xt[:])


def run(inp, layers, n_cores, n_ctx, n_lat, final_norm=True, gather=None):
    gather = (n_cores > 1) if gather is None else gather
    Wall, off, R = pack_weights(inp, layers, n_lat)
    nc = build(layers, n_ctx // 128, n_lat // 128, off, R, gather, final_norm, n_cores)
    in_maps = []
    for b in range(n_cores):
        m = {"x_in": np.ascontiguousarray(np.concatenate([inp["ctx"][b, :n_ctx], inp["x"][b, :n_lat]], axis=0), dtype=np.float32),
             "cvec": np.ascontiguousarray(np.stack([inp["c"][b], inp["c_ctx"]]), dtype=np.float32)}
        if gather:
            m["wsh"] = np.ascontiguousarray(Wall[b * (R // n_cores):(b + 1) * (R // n_cores)])
        else:
            m["wall"] = Wall
        in_maps.append(m)
    import os
    if os.environ.get("KTRACE"):
        res = run_bass_kernel_spmd(nc, in_maps, core_ids=list(range(n_cores)), trace=True)
        print("EXEC_TIME_NS", res.exec_time_ns, flush=True)
    else:
        res = run_bass_kernel_spmd(nc, in_maps, core_ids=list(range(n_cores)))
    return np.stack([r["y_out"] for r in res.results], axis=0)


def kernel(**inputs):
    inp = {k: np.asarray(v) for k, v in inputs.items()}
    return run(inp, [0, 1, 2, 3], 8, 256, 8192, gather=False).astype(np.float32)
```
